# Optimizing a Trainium2 kernel written in Bass

```python
import math
import jax, jax.numpy as jnp
from jax import lax
import numpy as np

D_MODEL = 1024
BATCH = 2
SEQ = 8192
DEPTH = 1

CHUNK = 64
N_META = 16
QBLK = 128
SB_HEADS = 8
SB_HEAD_DIM = D_MODEL // SB_HEADS
DF_HEADS = 8
DF_HEAD_DIM = D_MODEL // (2 * DF_HEADS)
SB_W = SB_HEADS * SB_HEAD_DIM
DF_QK_W = DF_HEADS * 2 * DF_HEAD_DIM
DF_V_W = DF_HEADS * 2 * DF_HEAD_DIM
IN_W = 3 * SB_W + 2 * DF_QK_W + DF_V_W
D_FF = -(-(8 * D_MODEL) // (3 * 256)) * 256
EPS = 1e-6

kernel_name = 'hybrid_stickbreak_diffattn_block'


def rmsnorm(x, g):
    xf = x.astype(jnp.float32)
    y = xf * lax.rsqrt(jnp.mean(xf * xf, axis=-1, keepdims=True) + EPS)
    return (y * g.astype(jnp.float32)).astype(x.dtype)


def stick_breaking_block(q, k, v, qpos, kpos):
    z = jnp.einsum('bhqd,bhkd->bhqk', q, k).astype(jnp.float32) * (SB_HEAD_DIM ** -0.5)
    causal = kpos[None, :] < qpos[:, None]
    u = jnp.where(causal, jax.nn.log_sigmoid(-z), 0.0)
    rest = lax.cumsum(u, axis=3, reverse=True) - u
    log_a = jax.nn.log_sigmoid(z) + rest
    a = jnp.where(causal, jnp.exp(log_a), 0.0)
    return jnp.einsum('bhqk,bhkd->bhqd', a.astype(v.dtype), v)


def diff_attn_block(q, k, v, lam, slopes, qpos, kpos, qcid, kcid):
    s = jnp.einsum('bhcqd,bhckd->bhcqk', q, k).astype(jnp.float32) * (DF_HEAD_DIM ** -0.5)
    dist = jnp.abs(qpos[:, None] - kpos[None, :]).astype(jnp.float32)
    bias = -slopes[:, None, None] * dist
    mask = kcid[None, :] <= qcid[:, None]
    s = jnp.where(mask, s + bias[None, :, None], -jnp.inf)
    p = jax.nn.softmax(s, axis=-1)
    w = p[:, :, 0] - lam * p[:, :, 1]
    return jnp.einsum('bhqk,bhkd->bhqd', w.astype(v.dtype), v)


def setup_inputs(seed: int = 0) -> dict:
    key = jax.random.key(seed)
    ks = jax.random.split(key, 20)
    f32 = jnp.float32
    nrm = lambda k, shape, scale: (jax.random.normal(k, shape, f32) * scale)
    gain = lambda k, shape: 1.0 + 0.05 * jax.random.normal(k, shape, f32)
    return {
        'x': jax.random.normal(ks[0], (BATCH, SEQ, D_MODEL), f32),
        'meta': nrm(ks[1], (N_META, D_MODEL), 1.0),
        'norm_mix_g': gain(ks[2], (DEPTH, D_MODEL)),
        'w_in': nrm(ks[3], (DEPTH, D_MODEL, IN_W), D_MODEL ** -0.5),
        'w_gate': nrm(ks[4], (DEPTH, D_MODEL, 2 * D_MODEL), D_MODEL ** -0.5),
        'b_gate': nrm(ks[5], (DEPTH, 2 * D_MODEL), 0.1),
        'lam_q1': nrm(ks[6], (DEPTH, DF_HEAD_DIM), 0.1),
        'lam_k1': nrm(ks[7], (DEPTH, DF_HEAD_DIM), 0.1),
        'lam_q2': nrm(ks[8], (DEPTH, DF_HEAD_DIM), 0.1),
        'lam_k2': nrm(ks[9], (DEPTH, DF_HEAD_DIM), 0.1),
        'subln_g': gain(ks[10], (DEPTH, 2 * DF_HEAD_DIM)),
        'w_br_sb': nrm(ks[11], (DEPTH, SB_W, D_MODEL), SB_W ** -0.5),
        'w_br_df': nrm(ks[12], (DEPTH, DF_V_W, D_MODEL), DF_V_W ** -0.5),
        'w_out': nrm(ks[13], (DEPTH, D_MODEL, D_MODEL), D_MODEL ** -0.5),
        'norm_ffn_g': gain(ks[14], (DEPTH, D_MODEL)),
        'w_ffn_gate': nrm(ks[15], (DEPTH, D_MODEL, D_FF), D_MODEL ** -0.5),
        'w_ffn_up': nrm(ks[16], (DEPTH, D_MODEL, D_FF), D_MODEL ** -0.5),
        'w_ffn_down': nrm(ks[17], (DEPTH, D_FF, D_MODEL), D_FF ** -0.5),
        'norm_final_g': gain(ks[18], (D_MODEL,)),
    }


def reference(x, meta, norm_mix_g, w_in, w_gate, b_gate, lam_q1, lam_k1, lam_q2, lam_k2,
              subln_g, w_br_sb, w_br_df, w_out, norm_ffn_g, w_ffn_gate, w_ffn_up,
              w_ffn_down, norm_final_g):
    B, S, D = x.shape
    L = N_META + S
    L_pad = -(-L // QBLK) * QBLK
    h = jnp.concatenate([
        jnp.broadcast_to(meta[None].astype(x.dtype), (B, N_META, D)),
        x,
        jnp.zeros((B, L_pad - L, D), x.dtype)], axis=1)
    pos = jnp.arange(L_pad)
    cid = jnp.where(pos < N_META, 0, (pos - N_META) // CHUNK + 1)
    slopes = jnp.exp2(-8.0 * (jnp.arange(DF_HEADS) + 1) / DF_HEADS).astype(jnp.float32)
    splits = [SB_W, 2 * SB_W, 3 * SB_W, 3 * SB_W + DF_QK_W, 3 * SB_W + 2 * DF_QK_W]

    for l in range(DEPTH):
        lam_init = 0.8 - 0.6 * math.exp(-0.3 * l)
        xn = rmsnorm(h, norm_mix_g[l])
        proj = xn @ w_in[l]
        sb_q, sb_k, sb_v, df_q, df_k, df_v = jnp.split(proj, splits, axis=-1)
        to_heads = lambda t, nh, dh: t.reshape(B, L_pad, nh, dh).transpose(0, 2, 1, 3)
        sq = to_heads(sb_q, SB_HEADS, SB_HEAD_DIM)
        sk = to_heads(sb_k, SB_HEADS, SB_HEAD_DIM)
        sv = to_heads(sb_v, SB_HEADS, SB_HEAD_DIM)
        dq = df_q.reshape(B, L_pad, DF_HEADS, 2, DF_HEAD_DIM).transpose(0, 2, 3, 1, 4)
        dk = df_k.reshape(B, L_pad, DF_HEADS, 2, DF_HEAD_DIM).transpose(0, 2, 3, 1, 4)
        dv = to_heads(df_v, DF_HEADS, 2 * DF_HEAD_DIM)
        lam = (jnp.exp(jnp.sum(lam_q1[l].astype(jnp.float32) * lam_k1[l].astype(jnp.float32)))
               - jnp.exp(jnp.sum(lam_q2[l].astype(jnp.float32) * lam_k2[l].astype(jnp.float32)))
               + lam_init)

        sb_blocks = []
        df_blocks = []
        for i in range(L_pad // QBLK):
            q0 = i * QBLK
            qe = q0 + QBLK
            lk = min(qe + CHUNK, L_pad)
            sb_blocks.append(stick_breaking_block(
                sq[:, :, q0:qe], sk[:, :, :qe], sv[:, :, :qe], pos[q0:qe], pos[:qe]))
            df_blocks.append(diff_attn_block(
                dq[:, :, :, q0:qe], dk[:, :, :, :lk], dv[:, :, :lk], lam, slopes,
                pos[q0:qe], pos[:lk], cid[q0:qe], cid[:lk]))
        o_sb = jnp.concatenate(sb_blocks, axis=2).transpose(0, 2, 1, 3).reshape(B, L_pad, SB_W)
        o_df = jnp.concatenate(df_blocks, axis=2)
        o_df = (rmsnorm(o_df, subln_g[l]) * (1.0 - lam_init)).transpose(0, 2, 1, 3).reshape(B, L_pad, DF_V_W)

        y_sb = o_sb @ w_br_sb[l]
        y_df = o_df @ w_br_df[l]
        gates = jax.nn.sigmoid(xn @ w_gate[l] + b_gate[l]).reshape(B, L_pad, 2, D)
        merged = gates[:, :, 0] * y_sb + gates[:, :, 1] * y_df
        h = h + merged @ w_out[l]

        hn = rmsnorm(h, norm_ffn_g[l])
        h = h + (jax.nn.silu(hn @ w_ffn_gate[l]) * (hn @ w_ffn_up[l])) @ w_ffn_down[l]

    y = rmsnorm(h, norm_final_g)
    return y[:, N_META:N_META + S]
```

```python
import contextlib
import math

import ml_dtypes
import numpy as np

import concourse.bass as bass
import concourse.mybir as mybir
from concourse.bass_utils import run_bass_kernel_spmd

F32 = mybir.dt.float32
BF = mybir.dt.bfloat16
AF = mybir.ActivationFunctionType
ALU = mybir.AluOpType
AX = mybir.AxisListType

D = 1024
S = 8192
NM = 16
L = S + NM
NST = 16
DFF = 2816
NFB = DFF // 128
EPS = 1e-6
SB_SCALE = 128 ** -0.5
DF_SCALE = 64 ** -0.5
LAM_INIT = 0.8 - 0.6 * math.exp(0.0)
BIGNEG = -30000.0
NCH = 8
SAME_ENG_SYNC = ("act", "dve", "pool")


def M(name, *a, **kw):
    return lambda e: getattr(e, name)(*a, **kw)


class Buf:
    def __init__(self, name):
        self.name = name
        self.w = None
        self.r = []
        self.dsem = None
        self.dcnt = 0


class Eng:
    def __init__(self, name):
        self.name = name
        self.sem = "e_" + name
        self.cnt = 0
        self.q = []


class Tracker:
    def __init__(self):
        self.eng = {n: Eng(n) for n in ("pe", "act", "dve", "pool", "sp")}
        self.semnames = [e.sem for e in self.eng.values()]
        self.bufs = []

    def buf(self, name):
        b = Buf(name)
        self.bufs.append(b)
        return b

    def _collect(self, eng, reads, writes):
        waits = {}

        def add(tok):
            if tok is None:
                return
            sem, val = tok
            if sem == eng.sem and eng.name not in SAME_ENG_SYNC:
                return
            if waits.get(sem, 0) < val:
                waits[sem] = val

        for b in reads:
            add(b.w)
        for b in writes:
            add(b.w)
            for t in b.r:
                add(t)
        return waits

    def _update(self, tok, reads, writes):
        for b in reads:
            b.r.append(tok)
        for b in writes:
            b.w = tok
            b.r = []

    def op(self, en, fn, reads=(), writes=()):
        eng = self.eng[en]
        waits = self._collect(eng, reads, writes)
        eng.cnt += 1
        tok = (eng.sem, eng.cnt)
        eng.q.append((waits, fn, (eng.sem, 1)))
        self._update(tok, reads, writes)
        return tok

    def dma(self, en, fn, sb, reads=(), writes=()):
        eng = self.eng[en]
        if sb.dsem is None:
            sb.dsem = "d_%d_%s" % (len(self.semnames), sb.name)
            self.semnames.append(sb.dsem)
        waits = self._collect(eng, reads, writes)
        sb.dcnt += 16
        tok = (sb.dsem, sb.dcnt)
        eng.q.append((waits, fn, (sb.dsem, 16)))
        self._update(tok, reads, writes)
        return tok

    def cc(self, fn, sem, reads=(), writes=()):
        eng = self.eng["pool"]
        waits = self._collect(eng, reads, writes)
        if sem not in self.semnames:
            self.semnames.append(sem)
        tok = (sem, 1)
        eng.q.append((waits, fn, (sem, 1)))
        self._update(tok, reads, writes)
        return tok

    def barrier(self):
        toks = {}
        for e in self.eng.values():
            if e.cnt:
                toks[e.sem] = e.cnt
        for b in self.bufs:
            for t in [b.w] + b.r:
                if t is not None and toks.get(t[0], 0) < t[1]:
                    toks[t[0]] = t[1]
        for e in self.eng.values():
            w = {s: v for s, v in toks.items() if s != e.sem}
            e.q.append((w, None, None))

    def replay(self, en, e, sems):
        eng = self.eng[en]
        known = {}
        for waits, fn, inc in eng.q:
            for s, v in waits.items():
                if known.get(s, 0) < v:
                    e.wait_ge(sems[s], v)
                    known[s] = v
            if fn is None:
                continue
            ins = fn(e)
            ins.then_inc(sems[inc[0]], inc[1])


def _consts(g):
    bf = ml_dtypes.bfloat16
    i = np.arange(128)
    k = i[:, None]
    q = i[None, :]
    ident = (k == q).astype(np.float32)
    tri_i = (k >= q).astype(np.float32)
    tri_c = (k < q).astype(np.float32)
    negm = np.where(k >= q, BIGNEG, 0.0)
    bigp = np.where(k > q, 100.0, 0.0)
    sh = (k == q - 1).astype(np.float32) - (k == q).astype(np.float32)
    sel = ((k == 127) & (q == 0)).astype(np.float32)
    selm = ((k == 15) & (q == 0)).astype(np.float32)
    cb = np.zeros((128, 2048 + 512), np.float32)
    cb[:, 0:128] = ident
    cb[:, 128:256] = tri_i
    cb[:, 256:384] = tri_c
    cb[:, 384:512] = negm
    cb[:, 512:640] = bigp
    cb[:, 640:768] = -bigp
    cb[:, 768:896] = sh
    cb[:, 896:1024] = sel
    cb[:, 1024:1152] = selm
    c3 = np.zeros((3, 2 * 656), np.float32)
    cf = np.zeros((128, 128 + 2 * 80), np.float32)
    cf[:, 0:128] = 1.0
    for hh in range(2):
        h = 2 * g + hh
        slope = 2.0 ** (-(h + 1))
        c = slope / DF_SCALE
        bc = np.where(k > q, -2.0 * c * (k - q), 0.0)
        bc = np.where((k >= 64) & (q < 64), BIGNEG, bc)
        cb[:, 1152 + 128 * hh:1280 + 128 * hh] = bc
        o = 656 * hh
        c3[0, o:o + 128] = c * i
        c3[1, o:o + 128] = 1.0
        c3[2, o:o + 128] = 1.0
        c3[0, o + 128:o + 144] = c * (np.arange(16) - 16)
        c3[1, o + 128:o + 144] = 1.0
        c3[2, o + 128:o + 144] = 1.0
        j = np.arange(512)
        c3[0, o + 144:o + 656] = 1.0
        c3[1, o + 144:o + 656] = -c * (j % 128)
        c3[2, o + 144:o + 656] = -c * 128 * (j // 128)
        for dl in range(-3, 70):
            cf[:, 128 + 80 * hh + dl + 3] = -slope * 128.0 * dl
    kb = np.zeros((2, 3, L), np.float32)
    pos = np.arange(L)
    ii = np.where(pos < NM, pos - NM, (pos - NM) % 128)
    for hh in range(2):
        h = 2 * g + hh
        c = (2.0 ** (-(h + 1))) / DF_SCALE
        kb[hh, 0] = c * ii
        kb[hh, 1] = 1.0
        kb[hh, 2] = 1.0
    return cb.astype(bf), c3.astype(bf), cf, kb.astype(bf)


def _pk(v, n):
    return np.ascontiguousarray(np.asarray(v, np.float32).reshape(n, 128).T)


def build(stop=None):
    nc = bass.Bass("TRN2", target_bir_lowering=False)
    T = Tracker()

    def din(name, shape, dt=F32):
        return nc.dram_tensor(name, list(shape), dt, kind="ExternalInput").ap()

    xb_d = din("xb", [S, D])
    meta_d = din("meta", [NM, D])
    x2_d = din("x2", [2048, D])
    w1_d = din("w1", [D, 1536])
    gmix_d = din("gmix", [128, 8])
    wgate_d = din("wgate", [D, 2048])
    bgate_d = din("bgate", [128, 16])
    wsb_d = din("wsb", [D, D])
    wdf_d = din("wdf", [D, D])
    wout_d = din("wout", [D, D])
    gffn_d = din("gffn", [128, 8])
    wfg_d = din("wfg", [D, DFF])
    wfu_d = din("wfu", [D, DFF])
    wfd_d = din("wfd", [DFF, D])
    gfin_d = din("gfin", [128, D])
    lam_d = din("lam", [128, 256])
    subg_d = din("subg", [128, 1])
    cb_d = din("cb", [128, 2560], BF)
    c3_d = din("c3", [3, 1312], BF)
    cf_d = din("cf", [128, 288])
    kb_d = din("kb", [2, 3, L], BF)
    y_d = nc.dram_tensor("y", [2048, D], F32, kind="ExternalOutput").ap()
    arb_d = nc.dram_tensor("arb", [NCH * 4096, 1024], BF).ap()
    arg_d = nc.dram_tensor("arg", [NCH * 4096, 1024], BF).ap()
    hbuf_d = nc.dram_tensor("hbuf", [2048, D], F32).ap()
    own_d = nc.dram_tensor("own", [NCH * 512, 1024], BF).ap()
    osel_d = nc.dram_tensor("osel", [2 * 2048, 1024], BF).ap()

    B_arb = [T.buf("arb%d" % c) for c in range(NCH)]
    B_arg = [T.buf("arg%d" % c) for c in range(NCH)]
    B_hbuf = [T.buf("hbuf%d" % c) for c in range(16)]
    B_y = T.buf("y")
    B_own = [T.buf("own%d" % c) for c in range(NCH)]
    B_ownx = [T.buf("ownx%d" % c) for c in range(NCH)]
    B_osel = [T.buf("osel%d" % c) for c in range(2)]

    es = contextlib.ExitStack()

    def sb(name, shape, dt, stack=None):
        t = (stack or es).enter_context(nc.sbuf_tensor("s_" + name, list(shape), dt))
        return t, T.buf(name)

    def ps(name, shape, dt, stack=None):
        t = (stack or es).enter_context(nc.psum_tensor("p_" + name, list(shape), dt))
        return t, T.buf(name)

    pidc = {}

    def PID(e):
        if "p" not in pidc:
            pidc["p"] = e.partition_id()
        return pidc["p"]


    def finish():
        T.barrier()
        with contextlib.ExitStack() as ss:
            sems = {n: ss.enter_context(nc.semaphore(n)) for n in T.semnames}
            block = ss.enter_context(nc.Block())

            @block.tensor
            def _(e):
                T.replay("pe", e, sems)

            @block.scalar
            def _(e):
                T.replay("act", e, sems)

            @block.vector
            def _(e):
                T.replay("dve", e, sems)

            @block.gpsimd
            def _(e):
                T.replay("pool", e, sems)

            @block.sync
            def _(e):
                T.replay("sp", e, sems)

    def dump(name, src, B_src, rows, cols, dt):
        d = nc.dram_tensor(name, [rows, cols], dt, kind="ExternalOutput").ap()
        T.dma("sp", M("dma_start", out=d[:, :], in_=src), T.buf(name + "_s"), reads=[B_src], writes=[T.buf(name)])

    cb, B_cb = sb("cb", [128, 2560], BF)
    c3, B_c3 = sb("c3", [3, 1312], BF)
    cf, B_cf = sb("cf", [128, 288], F32)
    lamt, B_lam = sb("lamt", [128, 256], F32)
    lamw, B_lamw = sb("lamw", [128, 8], F32)
    subg, B_subg = sb("subg", [128, 1], F32)
    gmix, B_gmix = sb("gmix", [128, 8], F32)
    gffn, B_gffn = sb("gffn", [128, 8], F32)
    bgate, B_bgate = sb("bgate", [128, 16], F32)
    stat, B_stat = sb("stat", [128, 8], F32)
    xt = [sb("xt%d" % i, [128, D], F32) for i in range(2)]
    xnb, B_xnb = sb("xnb", [128, D], BF)
    sqj, B_sqj = sb("sqj", [128, D], BF)
    wstg = [sb("wstg%d" % i, [128, 1024], F32) for i in range(2)]
    PBW = [ps("pbw%d" % i, [128, 2, 512], F32) for i in range(3)]
    PB = [(PBW[i // 2][0][:, i % 2, :], T.buf("pbh%d" % i)) for i in range(6)] + [ps("pb6", [128, 512], F32)]
    tp, B_tp = ps("tp", [128, 8, 128], BF)

    ident = cb[:, 0:128]
    triI = cb[:, 128:256]
    triC = cb[:, 256:384]
    negm = cb[:, 384:512]
    bigp = cb[:, 512:640]
    bigpn = cb[:, 640:768]
    shm = cb[:, 768:896]
    selc = cb[:, 896:1024]
    selmc = cb[:, 1024:1152]
    zer = cb[:, 2048:2560]
    onesf = cf[:, 0:128]

    T.dma("sp", M("dma_start", out=cb[:, :], in_=cb_d[:, :]), B_cb, writes=[B_cb])
    T.dma("sp", M("dma_start", out=c3[:, :], in_=c3_d[:, :]), B_c3, writes=[B_c3])
    T.dma("sp", M("dma_start", out=cf[:, :], in_=cf_d[:, :]), B_cf, writes=[B_cf])
    T.dma("sp", M("dma_start", out=lamt[:, :], in_=lam_d[:, :]), B_lam, writes=[B_lam])
    T.dma("sp", M("dma_start", out=subg[:, :], in_=subg_d[:, :]), B_subg, writes=[B_subg])
    T.dma("sp", M("dma_start", out=gmix[:, :], in_=gmix_d[:, :]), B_gmix, writes=[B_gmix])
    T.dma("sp", M("dma_start", out=gffn[:, :], in_=gffn_d[:, :]), B_gffn, writes=[B_gffn])
    T.dma("sp", M("dma_start", out=bgate[:, :], in_=bgate_d[:, :]), B_bgate, writes=[B_bgate])

    T.op("dve", M("tensor_tensor", out=lamt[:, 0:64], in0=lamt[:, 0:64], in1=lamt[:, 64:128], op=ALU.mult),
         reads=[B_lam], writes=[B_lam])
    T.op("dve", M("tensor_tensor", out=lamt[:, 128:192], in0=lamt[:, 128:192], in1=lamt[:, 192:256], op=ALU.mult),
         reads=[B_lam], writes=[B_lam])
    T.op("dve", M("reduce_sum", out=lamw[:, 0:1], in_=lamt[:, 0:64], axis=AX.X), reads=[B_lam], writes=[B_lamw])
    T.op("dve", M("reduce_sum", out=lamw[:, 1:2], in_=lamt[:, 128:192], axis=AX.X), reads=[B_lam], writes=[B_lamw])
    T.op("act", M("activation", out=lamw[:, 2:4], in_=lamw[:, 0:2], func=AF.Exp), reads=[B_lamw], writes=[B_lamw])
    T.op("dve", M("tensor_tensor", out=lamw[:, 4:5], in0=lamw[:, 3:4], in1=lamw[:, 2:3], op=ALU.subtract),
         reads=[B_lamw], writes=[B_lamw])
    T.op("dve", M("tensor_scalar", out=lamw[:, 4:5], in0=lamw[:, 4:5], scalar1=-LAM_INIT, scalar2=None, op0=ALU.add),
         reads=[B_lamw], writes=[B_lamw])
    T.op("dve", M("tensor_scalar", out=subg[:, 0:1], in0=subg[:, 0:1], scalar1=1.0 - LAM_INIT, scalar2=None, op0=ALU.mult),
         reads=[B_subg], writes=[B_subg])

    wq = {"i": 0}

    def load_w(src, k0, nk, c0, ncol, dst, B_dst, dk0, dc0, gain):
        for kc in range(nk):
            for cc in range(0, ncol, 1024):
                w = min(1024, ncol - cc)
                st, B_st = wstg[wq["i"] % 2]
                wq["i"] += 1
                r0 = (k0 + kc) * 128
                a = c0 + cc
                T.dma("sp", M("dma_start", out=st[:, 0:w], in_=src[r0:r0 + 128, a:a + w]),
                      B_st, writes=[B_st])
                if gain is None:
                    T.op("dve", M("tensor_copy",
                        out=dst[:, dk0 + kc, dc0 + cc:dc0 + cc + w], in_=st[:, 0:w]), reads=[B_st], writes=[B_dst])
                else:
                    gt, B_g = gain
                    T.op("dve", M("tensor_scalar",
                        out=dst[:, dk0 + kc, dc0 + cc:dc0 + cc + w], in0=st[:, 0:w],
                        scalar1=gt[:, k0 + kc:k0 + kc + 1], scalar2=None, op0=ALU.mult),
                        reads=[B_st, B_g], writes=[B_dst])

    xq = {"i": 0}

    def norm_tile(src_fn, rows, dst, B_dst, col0, src_reads=()):
        xtile, B_x = xt[xq["i"] % 2]
        xq["i"] += 1
        src_fn(xtile, B_x)
        T.op("act", M("activation", out=sqj[0:rows, :], in_=xtile[0:rows, :], func=AF.Square,
                                           accum_out=stat[0:rows, 0:1]), reads=[B_x], writes=[B_sqj, B_stat])
        T.op("act", M("activation", out=stat[0:rows, 1:2], in_=stat[0:rows, 0:1], func=AF.Ln,
                                           scale=1.0 / D, bias=cf[0:rows, 280:281]), reads=[B_stat, B_cf], writes=[B_stat])
        T.op("act", M("activation", out=stat[0:rows, 2:3], in_=stat[0:rows, 1:2], func=AF.Exp, scale=-0.5),
             reads=[B_stat], writes=[B_stat])
        T.op("dve", M("tensor_scalar", out=xnb[0:rows, :], in0=xtile[0:rows, :], scalar1=stat[0:rows, 2:3],
                                              scalar2=None, op0=ALU.mult), reads=[B_x, B_stat], writes=[B_xnb])
        for c in range(8):
            T.op("pe", M("transpose", out=tp[:, c, 0:rows], in_=xnb[0:rows, c * 128:(c + 1) * 128],
                                                 identity=ident[0:rows, 0:rows]), reads=[B_xnb, B_cb], writes=[B_tp])
        T.op("dve", M("tensor_copy", out=dst[:, :, col0:col0 + rows], in_=tp[:, :, 0:rows]),
             reads=[B_tp], writes=[B_dst])
        return xtile, B_x

    p1 = contextlib.ExitStack()
    GRPS = list(range(-1, NST))
    wb, B_wb = sb("wb", [128, 8, 768], BF, p1)
    KT4 = [sb("KT%d" % a, [128, L], BF, p1)[0] for a in range(4)]
    KTB = [{g_: T.buf("KT%d_%d" % (a, g_)) for g_ in GRPS} for a in range(4)]
    TMt = [sb("TM%d" % h, [128, 65, 128], BF, p1)[0] for h in range(2)]
    TMB = [{g_: T.buf("TM%d_%d" % (h, g_)) for g_ in GRPS} for h in range(2)]
    QT = [[sb("QT%d_%d" % (h, par), [128, 512], BF, p1) for par in range(2)] for h in range(2)]
    VT = [[sb("VT%d_%d" % (h, par), [128, 512], BF, p1) for par in range(2)] for h in range(2)]
    QTd = [[[sb("QTd%d_%d_%d" % (h, c, par), [128, 512], BF, p1) for par in range(2)] for c in range(2)] for h in range(2)]
    xnT, B_xnT = sb("xnT", [128, 8, 512], BF, p1)
    vtok = [sb("vtok%d" % i, [128, 256], BF, p1) for i in range(2)]
    UW, B_UW = sb("UW", [128, 2, 512], BF, p1)
    PW, B_PW = sb("PW", [128, 2, 512], BF, p1)
    eW, B_eW = sb("eW", [128, 2, 512], F32, p1)
    EW = [sb("EW%d" % par, [128, 2, 512], BF, p1) for par in range(2)]
    EaccW = [sb("EaccW%d" % h, [128, 2, 512], F32, p1) for h in range(2)]
    odsb = [[sb("od%d_%d" % (h, c), [128, 512], F32, p1) for c in range(2)] for h in range(2)]
    osb = [sb("osb%d" % h, [128, 512], BF, p1) for h in range(2)]
    t0, B_t0 = sb("t0", [128, 512], F32, p1)
    t1, B_t1 = sb("t1", [128, 512], F32, p1)
    t2, B_t2 = sb("t2", [128, 512], F32, p1)
    zt, B_zt = sb("zt", [128, 1024], BF, p1)
    wcv = [sb("wcv%d" % i, [128, 1024], BF, p1) for i in range(2)]

    def grp_of_kt(kt):
        return -1 if kt is None else kt // 4

    T.op("dve", M("memset", cf[:, 280:281], EPS), reads=[B_cf], writes=[B_cf])
    T.op("dve", M("memset", zt[:, :], 0.0), writes=[B_zt])
    if stop is not None:
        for h in range(2):
            T.op("dve", M("memset", TMt[h][:, 0, :], 0.0), writes=[TMB[h][-1]])

    ztok = None
    for c in range(NCH if stop is None else 0):
        for blk in range(32):
            r0 = c * 4096 + blk * 128
            ztok = T.dma("pool", M("dma_start", out=arb_d[r0:r0 + 128, :], in_=zt[:, :]), B_zt,
                         reads=[B_zt])
    for c in range(NCH if stop is None else 0):
        B_arb[c].w = ztok

    bg = []

    def pump(k=1):
        for _ in range(k):
            for g_ in list(bg):
                try:
                    next(g_)
                except StopIteration:
                    bg.remove(g_)

    def drain(g_):
        for _ in g_:
            pass
        if g_ in bg:
            bg.remove(g_)

    def norm_gen(src_fn, rows, dst, B_dst, col0):
        xtile, B_x = xt[xq["i"] % 2]
        xq["i"] += 1
        src_fn(xtile, B_x)
        yield
        T.op("act", M("activation", out=sqj[0:rows, :], in_=xtile[0:rows, :], func=AF.Square,
                      accum_out=stat[0:rows, 0:1]), reads=[B_x], writes=[B_sqj, B_stat])
        T.op("act", M("activation", out=stat[0:rows, 1:2], in_=stat[0:rows, 0:1], func=AF.Ln,
                      scale=1.0 / D, bias=cf[0:rows, 280:281]), reads=[B_stat, B_cf], writes=[B_stat])
        T.op("act", M("activation", out=stat[0:rows, 2:3], in_=stat[0:rows, 1:2], func=AF.Exp, scale=-0.5),
             reads=[B_stat], writes=[B_stat])
        yield
        T.op("dve", M("tensor_scalar", out=xnb[0:rows, :], in0=xtile[0:rows, :], scalar1=stat[0:rows, 2:3],
                      scalar2=None, op0=ALU.mult), reads=[B_x, B_stat], writes=[B_xnb])
        yield
        for c in range(8):
            T.op("pe", M("transpose", out=tp[:, c, 0:rows], in_=xnb[0:rows, c * 128:(c + 1) * 128],
                         identity=ident[0:rows, 0:rows]), reads=[B_xnb, B_cb], writes=[B_tp])
        yield
        T.op("dve", M("tensor_copy", out=dst[:, :, col0:col0 + rows], in_=tp[:, :, 0:rows]),
             reads=[B_tp], writes=[B_dst])
        yield

    def front_gen(pass_id, grp):
        if grp < 0:
            tiles = [(16, 0)]
            n = 16
            pos0 = 0
        else:
            tiles = [(128, j * 128) for j in range(4)]
            n = 512
            pos0 = NM + 512 * grp
        par = grp % 2
        for j, (rows, col0) in enumerate(tiles):
            if grp < 0:
                src = lambda xtile, B_x: T.dma("sp", M("dma_start", out=xtile[0:16, :], in_=meta_d[:, :]), B_x, writes=[B_x])
            else:
                r0 = 512 * grp + 128 * j
                src = lambda xtile, B_x, r0=r0: T.dma("sp", M("dma_start", out=xtile[:, :], in_=xb_d[r0:r0 + 128, :]), B_x, writes=[B_x])
            yield from norm_gen(src, rows, xnT, B_xnT, col0)
        pj, B_pj = PB[6]
        for h in range(2):
            for kind in range(2 if pass_id == 1 else 0):
                for c in range(2):
                    c0 = (2 * h + kind) * 128 + 64 * c
                    if kind == 0 and grp < 0:
                        continue
                    for dc in range(8):
                        T.op("pe", M("matmul", out=pj[0:64, 0:n], lhsT=wb[:, dc, c0:c0 + 64], rhs=xnT[:, dc, 0:n],
                                     start=(dc == 0), stop=(dc == 7)), reads=[B_wb, B_xnT], writes=[B_pj])
                    yield
                    if kind == 0:
                        qd, B_qd = QTd[h][c][par]
                        T.op("dve", M("tensor_copy", out=qd[0:64, 0:n], in_=pj[0:64, 0:n]), reads=[B_pj], writes=[B_qd])
                    else:
                        T.op("dve", M("tensor_copy", out=KT4[2 * h + c][0:64, pos0:pos0 + n], in_=pj[0:64, 0:n]),
                             reads=[B_pj], writes=[KTB[2 * h + c][grp]])
                    yield
            for kind in range(2 if pass_id == 0 else 0):
                c0 = (2 * h + kind) * 128
                if kind == 0 and grp < 0:
                    continue
                for dc in range(8):
                    T.op("pe", M("matmul", out=pj[:, 0:n], lhsT=wb[:, dc, c0:c0 + 128], rhs=xnT[:, dc, 0:n],
                                 start=(dc == 0), stop=(dc == 7)), reads=[B_wb, B_xnT], writes=[B_pj])
                yield
                if kind == 0:
                    qq, B_qq = QT[h][par]
                    T.op("dve", M("tensor_copy", out=qq[:, 0:n], in_=pj[:, 0:n]), reads=[B_pj], writes=[B_qq])
                else:
                    T.op("dve", M("tensor_copy", out=KT4[h][:, pos0:pos0 + n], in_=pj[:, 0:n]), reads=[B_pj], writes=[KTB[h][grp]])
                yield
            if pass_id == 0 and grp >= 0:
                c0 = 512 + h * 128
                for dc in range(8):
                    T.op("pe", M("matmul", out=pj[:, 0:n], lhsT=wb[:, dc, c0:c0 + 128], rhs=xnT[:, dc, 0:n],
                                 start=(dc == 0), stop=(dc == 7)), reads=[B_wb, B_xnT], writes=[B_pj])
                yield
                vv, B_vv = VT[h][par]
                T.op("dve", M("tensor_copy", out=vv[:, 0:n], in_=pj[:, 0:n]), reads=[B_pj], writes=[B_vv])
                yield
        for j, (rows, col0) in enumerate(tiles):
            ti = 0 if grp < 0 else 1 + 4 * grp + j
            for dc in range(8):
                T.op("pe", M("matmul", out=pj[0:rows, 0:256], lhsT=xnT[:, dc, col0:col0 + rows], rhs=wb[:, dc, 512:768],
                             start=(dc == 0), stop=(dc == 7)), reads=[B_wb, B_xnT], writes=[B_pj])
            yield
            if pass_id == 1:
                for h in range(2):
                    T.op("dve", M("tensor_copy", out=TMt[h][0:rows, ti, :], in_=pj[0:rows, h * 128:(h + 1) * 128]),
                         reads=[B_pj], writes=[TMB[h][grp]])
                yield
            else:
                vc, B_vc = vtok[ti % 2]
                vp, B_vp = vtok[(ti + 1) % 2]
                T.op("dve", M("tensor_copy", out=vc[0:rows, :], in_=pj[0:rows, 0:256]), reads=[B_pj], writes=[B_vc])
                yield
                T.op("pe", M("matmul", out=pj[0:rows, 256:512], lhsT=shm[0:rows, 0:rows], rhs=vc[0:rows, :],
                             start=True, stop=True), reads=[B_vc, B_cb], writes=[B_pj])
                if ti == 1:
                    T.op("pe", M("matmul", out=pj[:, 256:512], lhsT=selmc[0:16, :], rhs=vp[0:16, :], start=False, stop=True,
                                 skip_group_check=True), reads=[B_vp, B_cb], writes=[B_pj])
                elif ti > 1:
                    T.op("pe", M("matmul", out=pj[:, 256:512], lhsT=selc[:, :], rhs=vp[:, :], start=False, stop=True,
                                 skip_group_check=True), reads=[B_vp, B_cb], writes=[B_pj])
                yield
                for h in range(2):
                    T.op("dve", M("tensor_copy", out=TMt[h][0:rows, ti, :], in_=pj[0:rows, 256 + h * 128:256 + (h + 1) * 128]),
                         reads=[B_pj], writes=[TMB[h][grp]])
                yield

    def key_tiles(s):
        lst = []
        for kt in range(4 * s + 3, -1, -1):
            a = kt - 4 * s
            lst.append((128, NM + 128 * kt, 1 + kt, 128 * a if a >= 0 else 0, a >= 0, kt))
        lst.append((16, 0, 0, 0, False, None))
        return lst

    def store_o(s, row_in_slot, src, B_src):
        c = s // 2
        col = (s % 2) * 512
        r0 = c * 512 + row_in_slot
        T.dma("pool", M("dma_start", out=own_d[r0:r0 + 128, col:col + 512], in_=src[:, :]),
              B_src, reads=[B_src], writes=[B_own[c]])

    WB = {}

    def wconv_gen(name, src, K, N, gain):
        dst = nc.dram_tensor(name, [K, N], BF).ap()
        B_d = T.buf(name)
        WB[name] = (dst, B_d)
        i = 0
        for kc in range(K // 128):
            for cc in range(0, N, 1024):
                w = min(1024, N - cc)
                st, B_st = wstg[wq["i"] % 2]
                cv, B_cv = wcv[wq["i"] % 2]
                wq["i"] += 1
                r0 = kc * 128
                T.dma("sp", M("dma_start", out=st[:, 0:w], in_=src[r0:r0 + 128, cc:cc + w]), B_st, writes=[B_st])
                yield
                if gain is None:
                    T.op("dve", M("tensor_copy", out=cv[:, 0:w], in_=st[:, 0:w]), reads=[B_st], writes=[B_cv])
                else:
                    gt, B_g = gain
                    T.op("dve", M("tensor_scalar", out=cv[:, 0:w], in0=st[:, 0:w], scalar1=gt[:, kc:kc + 1], scalar2=None,
                                  op0=ALU.mult), reads=[B_st, B_g], writes=[B_cv])
                yield
                T.dma("sp", M("dma_start", out=dst[r0:r0 + 128, cc:cc + w], in_=cv[:, 0:w]), B_cv, reads=[B_cv], writes=[B_d])
                yield

    def wconv_all():
        yield from wconv_gen("wsb_bf", wsb_d, D, D, None)
        yield from wconv_gen("wdf_bf", wdf_d, D, D, None)
        yield from wconv_gen("wg_bf", wgate_d, D, 2048, (gmix, B_gmix))
        yield from wconv_gen("wo_bf", wout_d, D, D, None)
        yield from wconv_gen("wfg_bf", wfg_d, D, DFF, (gffn, B_gffn))
        yield from wconv_gen("wfu_bf", wfu_d, D, DFF, (gffn, B_gffn))
        yield from wconv_gen("wfd_bf", wfd_d, DFF, D, None)

    load_w(w1_d, 0, 8, 0, 512, wb, B_wb, 0, 0, (gmix, B_gmix))
    load_w(w1_d, 0, 8, 1024, 256, wb, B_wb, 0, 512, (gmix, B_gmix))

    def sb_attention(s, rate):
        tiles = key_tiles(s)
        par = s % 2
        zW, B_zW = PBW[0]
        RW, B_RW = PBW[1]
        oW, B_oW = PBW[2]
        for h in range(2):
            T.op("pe", M("matmul", out=RW[:, h, :], lhsT=zer[:, 0:128], rhs=zer[:, :], start=True, stop=True),
                 reads=[B_cb], writes=[B_RW])
            T.op("pe", M("matmul", out=oW[:, h, :], lhsT=zer[:, 0:128], rhs=zer[:, :], start=True, stop=True),
                 reads=[B_cb], writes=[B_oW])
        N = len(tiles)

        def qk(n):
            ksz, kc0, ti, q0, diag, kt = tiles[n]
            for h in range(2):
                qq, B_qq = QT[h][par]
                T.op("pe", M("matmul", out=zW[0:ksz, h, q0:512], lhsT=KT4[h][:, kc0:kc0 + ksz], rhs=qq[:, q0:512],
                             start=True, stop=True), reads=[KTB[h][grp_of_kt(kt)], B_qq], writes=[B_zW])
                if diag:
                    T.op("pe", M("matmul", out=zW[:, h, q0:q0 + 128], lhsT=ident, rhs=negm, start=False, stop=True,
                                 skip_group_check=True), reads=[B_cb], writes=[B_zW])

        def ex(n):
            ksz, kc0, ti, q0, diag, kt = tiles[n]
            T.op("act", M("activation", out=eW[0:ksz, :, q0:512], in_=zW[0:ksz, :, q0:512], func=AF.Exp, scale=SB_SCALE),
                 reads=[B_zW], writes=[B_eW])

        qk(0)
        ex(0)
        if N > 1:
            qk(1)
        for n in range(N):
            ksz, kc0, ti, q0, diag, kt = tiles[n]
            last = n == N - 1
            T.op("act", M("activation", out=UW[0:ksz, :, q0:512], in_=eW[0:ksz, :, q0:512], func=AF.Ln, bias=1.0),
                 reads=[B_eW], writes=[B_UW])
            for h in range(2):
                T.op("pe", M("matmul", out=RW[0:ksz, h, q0:512], lhsT=triI[0:ksz, 0:ksz], rhs=UW[0:ksz, h, q0:512],
                             start=False, stop=True, skip_group_check=True), reads=[B_UW, B_cb], writes=[B_RW])
                if diag:
                    T.op("pe", M("matmul", out=RW[:, h, q0:q0 + 128], lhsT=ident, rhs=bigp, start=False, stop=True,
                                 skip_group_check=True), reads=[B_cb], writes=[B_RW])
            if n + 1 < N:
                ex(n + 1)
            pump(rate)
            T.op("act", M("activation", out=PW[0:ksz, :, q0:512], in_=RW[0:ksz, :, q0:512], func=AF.Exp, scale=-1.0),
                 reads=[B_RW], writes=[B_PW])
            for h in range(2):
                if not last:
                    T.op("pe", M("matmul", out=RW[:, h, q0:512], lhsT=triC[:, :], rhs=UW[:, h, q0:512], start=False, stop=True,
                                 skip_group_check=True), reads=[B_UW, B_cb], writes=[B_RW])
                    if diag:
                        T.op("pe", M("matmul", out=RW[:, h, q0:q0 + 128], lhsT=ident, rhs=bigpn, start=False, stop=True,
                                     skip_group_check=True), reads=[B_cb], writes=[B_RW])
                T.op("pe", M("matmul", out=oW[:, h, q0:512], lhsT=TMt[h][0:ksz, ti, :], rhs=PW[0:ksz, h, q0:512],
                             start=False, stop=True, skip_group_check=True), reads=[TMB[h][grp_of_kt(kt)], B_PW], writes=[B_oW])
            if n + 2 < N:
                qk(n + 2)
        for h in range(2):
            ob, B_ob = osb[h]
            vv, B_vv = VT[h][par]
            T.op("dve", M("tensor_tensor", out=ob[:, :], in0=oW[:, h, :], in1=vv[:, :], op=ALU.add),
                 reads=[B_oW, B_vv], writes=[B_ob])
            store_o(s, h * 128, ob, B_ob)

    NFRONT = 64
    inter = stop is None or stop in ("sb1", "df1")
    stop_s = int(stop[2:]) if stop is not None and stop[:2] in ("sb", "df") else None
    sA = stop is not None and stop[:2] == "df"

    if not sA:
        drain(front_gen(0, -1))
        drain(front_gen(0, 0))
        if inter:
            bg.append(wconv_all())
    for s in range(NST if not sA else 0):
        if stop == "front":
            dump("dbg_kt", KT4[0][:, 0:528], KTB[0][0], 128, 528, BF)
            dump("dbg_tm", TMt[0][:, 0:5, :], TMB[0][0], 128, 640, BF)
            dump("dbg_qt", QT[0][0][0][:, :], QT[0][0][1], 128, 512, BF)
            dump("dbg_vt", VT[0][0][0][:, :], VT[0][0][1], 128, 512, F32)
            finish()
            return nc
        fg = None
        if s + 1 < NST and inter:
            fg = front_gen(0, s + 1)
            bg.insert(0, fg)
        nsteps = 4 * s + 5
        sb_attention(s, -(-NFRONT // nsteps))
        if fg is not None:
            drain(fg)
        if stop is not None and stop[:2] == "sb" and s == stop_s:
            dump("dbg_o0", osb[0][0][:, :], osb[0][1], 128, 512, BF)
            dump("dbg_o1", osb[1][0][:, :], osb[1][1], 128, 512, BF)
            finish()
            return nc
    for g_ in list(bg):
        drain(g_)

    load_w(w1_d, 0, 8, 512, 512, wb, B_wb, 0, 0, (gmix, B_gmix))
    load_w(w1_d, 0, 8, 1280, 256, wb, B_wb, 0, 512, (gmix, B_gmix))

    def df_attention(s, rate, prev_eg=None):
        tiles = key_tiles(s)
        par = s % 2
        zW, B_zW = PBW[0]
        oWs = [PBW[1], PBW[2]]
        for h in range(2):
            for c in range(2):
                T.op("pe", M("matmul", out=oWs[h][0][:, c, :], lhsT=zer[:, 0:128], rhs=zer[:, :], start=True, stop=True),
                     reads=[B_cb], writes=[oWs[h][1]])
            T.op("dve", M("memset", EaccW[h][0][:, :, :], 0.0), writes=[EaccW[h][1]])
        steps = [(n, h) for n in range(len(tiles)) for h in range(2)]

        def qk(m):
            n, h = steps[m]
            ksz, kc0, ti, q0, diag, kt = tiles[n]
            for c in range(2):
                qd, B_qd = QTd[h][c][par]
                T.op("pe", M("matmul", out=zW[0:ksz, c, q0:512], lhsT=KT4[2 * h + c][:, kc0:kc0 + ksz],
                             rhs=qd[:, q0:512], start=True, stop=True),
                     reads=[KTB[2 * h + c][grp_of_kt(kt)], B_qd], writes=[B_zW])
                if diag:
                    T.op("pe", M("matmul", out=zW[:, c, q0:q0 + 128], lhsT=ident, rhs=cb[:, 1152 + 128 * h:1280 + 128 * h],
                                 start=False, stop=True, skip_group_check=True), reads=[B_cb], writes=[B_zW])

        def av(m):
            n, h = steps[m]
            ksz, kc0, ti, q0, diag, kt = tiles[n]
            E, B_E = EW[m % 2]
            o, B_o = oWs[h]
            for c in range(2):
                T.op("pe", M("matmul", out=o[:, c, q0:512], lhsT=TMt[h][0:ksz, ti, :], rhs=E[0:ksz, c, q0:512],
                             start=False, stop=True, skip_group_check=True), reads=[TMB[h][grp_of_kt(kt)], B_E], writes=[B_o])

        qk(0)
        for m in range(len(steps)):
            n, h = steps[m]
            ksz, kc0, ti, q0, diag, kt = tiles[n]
            dl = (4 * s - kt) if kt is not None else 4 * s
            kcol = 128 + 80 * h + dl + 3
            E, B_E = EW[m % 2]
            T.op("act", M("activation", out=E[0:ksz, :, q0:512], in_=zW[0:ksz, :, q0:512], func=AF.Exp,
                          scale=DF_SCALE, bias=cf[0:ksz, kcol:kcol + 1]), reads=[B_zW, B_cf], writes=[B_E])
            ea, B_ea = EaccW[h]
            T.op("dve", M("tensor_tensor", out=ea[0:ksz, :, q0:512], in0=ea[0:ksz, :, q0:512], in1=E[0:ksz, :, q0:512], op=ALU.add),
                 reads=[B_E, B_ea], writes=[B_ea])
            if m + 1 < len(steps):
                qk(m + 1)
            av(m)
            if h == 1:
                pump(rate)
        if prev_eg is not None:
            drain(prev_eg)
        sm, B_sm = PB[6]
        for h in range(2):
            for c in range(2):
                ea, B_ea = EaccW[h]
                od, B_od = odsb[h][c]
                T.op("pe", M("matmul", out=sm[:, :], lhsT=onesf, rhs=ea[:, c, :], start=True, stop=True),
                     reads=[B_ea, B_cf], writes=[B_sm])
                T.op("dve", M("reciprocal", out=od[:, :], in_=sm[:, :]), reads=[B_sm], writes=[B_od])
                T.op("dve", M("tensor_tensor", out=od[:, :], in0=oWs[h][0][:, c, :], in1=od[:, :], op=ALU.mult),
                     reads=[oWs[h][1], B_od], writes=[B_od])

    def df_epilogue(s):
        sm, B_sm = PB[6]
        for h in range(2):
            T.op("dve", M("scalar_tensor_tensor", out=t0[:, :], in0=odsb[h][1][0][:, :], scalar=lamw[:, 4:5], in1=odsb[h][0][0][:, :],
                          op0=ALU.mult, op1=ALU.add), reads=[odsb[h][0][1], odsb[h][1][1], B_lamw], writes=[B_t0])
            T.op("dve", M("tensor_tensor", out=t1[:, :], in0=t0[:, :], in1=t0[:, :], op=ALU.mult), reads=[B_t0], writes=[B_t1])
            yield
            T.op("pe", M("matmul", out=sm[:, :], lhsT=onesf, rhs=t1[:, :], start=True, stop=True),
                 reads=[B_t1, B_cf], writes=[B_sm])
            yield
            T.op("act", M("activation", out=t2[:, :], in_=sm[:, :], func=AF.Ln, scale=1.0 / 128.0, bias=cf[:, 280:281]),
                 reads=[B_sm, B_cf], writes=[B_t2])
            T.op("act", M("activation", out=t2[:, :], in_=t2[:, :], func=AF.Exp, scale=-0.5), reads=[B_t2], writes=[B_t2])
            yield
            T.op("dve", M("tensor_tensor", out=t2[:, :], in0=t2[:, :], in1=t0[:, :], op=ALU.mult), reads=[B_t0, B_t2], writes=[B_t2])
            ob, B_ob = osb[h]
            T.op("dve", M("tensor_scalar", out=ob[:, :], in0=t2[:, :], scalar1=subg[:, 0:1], scalar2=None, op0=ALU.mult),
                 reads=[B_t2, B_subg], writes=[B_ob])
            store_o(s, 256 + h * 128, ob, B_ob)
            yield
        if s % 2 == 1 and stop is None:
            c = s // 2
            T.dma("pool", lambda e, c=c: e.dma_start(
                out=arb_d[bass.ds(PID(e) * 512 + c * 4096, 512), :], in_=own_d[c * 512:(c + 1) * 512, :]),
                B_ownx[c], reads=[B_own[c]], writes=[B_arb[c]])
            T.cc(M("collective_compute", "AllReduce", ALU.add, replica_groups=[list(range(8))],
                   ins=[arb_d[c * 4096:(c + 1) * 4096, :]], outs=[arg_d[c * 4096:(c + 1) * 4096, :]]),
                 "cc%d" % c, reads=[B_arb[c]], writes=[B_arg[c]])

    allg = [g_ for g_ in GRPS]
    for h in range(2):
        for c in range(2):
            T.op("dve", M("memset", KT4[2 * h + c][64:128, :], 0.0), writes=[KTB[2 * h + c][g_] for g_ in allg])
            T.dma("sp", M("dma_start", out=KT4[2 * h + c][64:67, :], in_=kb_d[h, :, :]), KTB[2 * h + c][-1],
                  writes=[KTB[2 * h + c][g_] for g_ in allg])
            for par in range(2):
                qd, B_qd = QTd[h][c][par]
                T.op("dve", M("memset", qd[64:128, :], 0.0), writes=[B_qd])
                T.dma("sp", M("dma_start", out=qd[64:67, :], in_=c3_d[:, 656 * h + 144:656 * h + 656]), B_qd, writes=[B_qd])
    drain(front_gen(1, -1))
    drain(front_gen(1, 0))
    prev_eg = None
    for s in range(NST):
        fg = None
        if s + 1 < NST and inter:
            fg = front_gen(1, s + 1)
            bg.insert(0, fg)
        nsteps = 4 * s + 5
        df_attention(s, -(-NFRONT // nsteps), prev_eg)
        if fg is not None:
            drain(fg)
        eg = df_epilogue(s)
        prev_eg = eg
        if sA and s == stop_s:
            drain(eg)
            dump("dbg_o0", osb[0][0][:, :], osb[0][1], 128, 512, BF)
            dump("dbg_o1", osb[1][0][:, :], osb[1][1], 128, 512, BF)
            finish()
            return nc
        bg.append(eg)
    for g_ in list(bg):
        drain(g_)

    T.barrier()
    p1.close()

    p2 = contextlib.ExitStack()
    Wsb, B_Wsb = sb("Wsb", [128, 8, 1024], BF, p2)
    Wdf, B_Wdf = sb("Wdf", [128, 8, 1024], BF, p2)
    Wg, B_Wg = sb("Wg", [128, 8, 2048], BF, p2)
    Wo, B_Wo = sb("Wo", [128, 8, 1024], BF, p2)
    oT, B_oT = sb("oT", [128, 16, 512], BF, p2)
    xnT2, B_xnT2 = sb("xnT2", [128, 8, 512], BF, p2)
    mgT, B_mgT = sb("mgT", [128, 8, 512], BF, p2)
    xk = [sb("xk%d" % j, [128, D], F32, p2) for j in range(4)]
    g0, B_g0 = sb("g0", [128, 512], F32, p2)
    g1, B_g1 = sb("g1", [128, 512], F32, p2)
    ht = [sb("ht%d" % j, [128, D], F32, p2) for j in range(2)]
    for kc_ in range(8):
        T.dma("sp", M("dma_start", out=Wsb[:, kc_, :], in_=WB["wsb_bf"][0][kc_ * 128:(kc_ + 1) * 128, :]), B_Wsb,
              reads=[WB["wsb_bf"][1]], writes=[B_Wsb])
    for kc_ in range(8):
        T.dma("sp", M("dma_start", out=Wdf[:, kc_, :], in_=WB["wdf_bf"][0][kc_ * 128:(kc_ + 1) * 128, :]), B_Wdf,
              reads=[WB["wdf_bf"][1]], writes=[B_Wdf])
    for kc_ in range(8):
        T.dma("sp", M("dma_start", out=Wg[:, kc_, :], in_=WB["wg_bf"][0][kc_ * 128:(kc_ + 1) * 128, :]), B_Wg,
              reads=[WB["wg_bf"][1]], writes=[B_Wg])
    for kc_ in range(8):
        T.dma("sp", M("dma_start", out=Wo[:, kc_, :], in_=WB["wo_bf"][0][kc_ * 128:(kc_ + 1) * 128, :]), B_Wo,
              reads=[WB["wo_bf"][1]], writes=[B_Wo])

    for u in range(4):
        for j in range(4):
            xkt, B_xk = xk[j]
            r0 = 512 * u + 128 * j

            def src(xtile, B_x, r0=r0, xkt=xkt, B_xk=B_xk):
                T.dma("sp", M("dma_start", out=xtile[:, :], in_=x2_d[r0:r0 + 128, :]), B_x, writes=[B_x])
                T.dma("sp", M("dma_start", out=xkt[:, :], in_=x2_d[r0:r0 + 128, :]), B_xk, writes=[B_xk])
            norm_tile(src, 128, xnT2, B_xnT2, 128 * j)
        col = (u % 2) * 512
        cc2 = u // 2
        if u % 2 == 0:
            T.dma("pool", lambda e, cc2=cc2: e.dma_start(
                out=osel_d[cc2 * 2048:(cc2 + 1) * 2048, :],
                in_=arg_d[bass.ds(((PID(e) % 4) * 2 + cc2) * 4096 + (PID(e) // 4) * 2048, 2048), :]),
                B_osel[cc2], reads=[B_arg[2 * i_ + cc2] for i_ in range(4)], writes=[B_osel[cc2]])
        T.dma("sp", M("dma_start",
            out=oT[:, :, :],
            in_=osel_d[cc2 * 2048:(cc2 + 1) * 2048, col:col + 512].rearrange("(k p) n -> p k n", p=128)),
            B_oT, reads=[B_osel[cc2]], writes=[B_oT])
        for blk in range(8):
            ysb, B_ysb = PB[0]
            ydf, B_ydf = PB[1]
            ga, B_ga = PB[2]
            gb, B_gb = PB[3]
            idx = 0
            for r in range(4):
                for jj in range(2):
                    hd = 2 * r + jj
                    T.op("pe", M("matmul",
                        out=ysb[:, :], lhsT=Wsb[:, hd, blk * 128:(blk + 1) * 128], rhs=oT[:, 4 * r + jj, :],
                        start=(idx == 0), stop=(idx == 7)), reads=[B_Wsb, B_oT], writes=[B_ysb])
                    idx += 1
            idx = 0
            for r in range(4):
                for jj in range(2):
                    hd = 2 * r + jj
                    T.op("pe", M("matmul",
                        out=ydf[:, :], lhsT=Wdf[:, hd, blk * 128:(blk + 1) * 128], rhs=oT[:, 4 * r + 2 + jj, :],
                        start=(idx == 0), stop=(idx == 7)), reads=[B_Wdf, B_oT], writes=[B_ydf])
                    idx += 1
            for dc in range(8):
                T.op("pe", M("matmul", out=ga[:, :], lhsT=Wg[:, dc, blk * 128:(blk + 1) * 128], rhs=xnT2[:, dc, :],
                                                     start=(dc == 0), stop=(dc == 7)), reads=[B_Wg, B_xnT2], writes=[B_ga])
            for dc in range(8):
                T.op("pe", M("matmul", out=gb[:, :], lhsT=Wg[:, dc, 1024 + blk * 128:1024 + (blk + 1) * 128],
                                                     rhs=xnT2[:, dc, :], start=(dc == 0), stop=(dc == 7)),
                     reads=[B_Wg, B_xnT2], writes=[B_gb])
            T.op("act", M("activation", out=g0[:, :], in_=ga[:, :], func=AF.Sigmoid, bias=bgate[:, blk:blk + 1]),
                 reads=[B_ga, B_bgate], writes=[B_g0])
            T.op("act", M("activation", out=g1[:, :], in_=gb[:, :], func=AF.Sigmoid, bias=bgate[:, 8 + blk:9 + blk]),
                 reads=[B_gb, B_bgate], writes=[B_g1])
            T.op("dve", M("tensor_tensor", out=g0[:, :], in0=g0[:, :], in1=ysb[:, :], op=ALU.mult), reads=[B_g0, B_ysb], writes=[B_g0])
            T.op("dve", M("tensor_tensor", out=g1[:, :], in0=g1[:, :], in1=ydf[:, :], op=ALU.mult), reads=[B_g1, B_ydf], writes=[B_g1])
            T.op("dve", M("tensor_tensor", out=mgT[:, blk, :], in0=g0[:, :], in1=g1[:, :], op=ALU.add),
                 reads=[B_g0, B_g1], writes=[B_mgT])
        for j in range(4):
            h_t, B_h = ht[j % 2]
            xkt, B_xk = xk[j]
            for half in range(2):
                mo, B_mo = PB[4 + half]
                for blk in range(8):
                    T.op("pe", M("matmul", out=mo[:, :], lhsT=mgT[:, blk, 128 * j:128 * j + 128],
                                                           rhs=Wo[:, blk, half * 512:(half + 1) * 512], start=(blk == 0), stop=(blk == 7)),
                         reads=[B_mgT, B_Wo], writes=[B_mo])
                T.op("dve", M("tensor_tensor", out=h_t[:, half * 512:(half + 1) * 512], in0=mo[:, :],
                                                             in1=xkt[:, half * 512:(half + 1) * 512], op=ALU.add),
                     reads=[B_mo, B_xk], writes=[B_h])
            r0 = 512 * u + 128 * j
            T.dma("sp", M("dma_start", out=hbuf_d[r0:r0 + 128, :], in_=h_t[:, :]), B_h,
                  reads=[B_h], writes=[B_hbuf[4 * u + j]])
    T.barrier()
    p2.close()

    p3 = contextlib.ExitStack()
    Wfg, B_Wfg = sb("Wfg", [128, 8, DFF], BF, p3)
    Wfu, B_Wfu = sb("Wfu", [128, 8, DFF], BF, p3)
    Wfd, B_Wfd = sb("Wfd", [128, NFB, 1024], BF, p3)
    gfin, B_gfin = sb("gfin", [128, D], F32, p3)
    hnT, B_hnT = sb("hnT", [128, 8, 256], BF, p3)
    hidT, B_hidT = sb("hidT", [128, NFB, 256], BF, p3)
    hk = [sb("hk%d" % j, [128, D], F32, p3) for j in range(2)]
    sg, B_sg = sb("sg", [128, 256], F32, p3)
    yo = [sb("yo%d" % j, [128, D], F32, p3) for j in range(2)]
    T.dma("sp", M("dma_start", out=gfin[:, :], in_=gfin_d[:, :]), B_gfin, writes=[B_gfin])
    for kc_ in range(8):
        T.dma("sp", M("dma_start", out=Wfg[:, kc_, :], in_=WB["wfg_bf"][0][kc_ * 128:(kc_ + 1) * 128, :]), B_Wfg,
              reads=[WB["wfg_bf"][1]], writes=[B_Wfg])
    for kc_ in range(8):
        T.dma("sp", M("dma_start", out=Wfu[:, kc_, :], in_=WB["wfu_bf"][0][kc_ * 128:(kc_ + 1) * 128, :]), B_Wfu,
              reads=[WB["wfu_bf"][1]], writes=[B_Wfu])
    for kc_ in range(22):
        T.dma("sp", M("dma_start", out=Wfd[:, kc_, :], in_=WB["wfd_bf"][0][kc_ * 128:(kc_ + 1) * 128, :]), B_Wfd,
              reads=[WB["wfd_bf"][1]], writes=[B_Wfd])

    for v in range(8):
        for j in range(2):
            hkt, B_hk = hk[j]
            r0 = 256 * v + 128 * j

            def src(xtile, B_x, r0=r0, hkt=hkt, B_hk=B_hk, v=v, j=j):
                T.dma("sp", M("dma_start", out=xtile[:, :], in_=hbuf_d[r0:r0 + 128, :]), B_x,
                      reads=[B_hbuf[2 * v + j]], writes=[B_x])
                T.dma("sp", M("dma_start", out=hkt[:, :], in_=hbuf_d[r0:r0 + 128, :]), B_hk,
                      reads=[B_hbuf[2 * v + j]], writes=[B_hk])
            norm_tile(src, 128, hnT, B_hnT, 128 * j)
        for fb in range(NFB):
            gp, B_gp = PB[fb % 2]
            up, B_up = PB[2 + fb % 2]
            for dc in range(8):
                T.op("pe", M("matmul", out=gp[:, 0:256], lhsT=Wfg[:, dc, fb * 128:(fb + 1) * 128], rhs=hnT[:, dc, :],
                                                            start=(dc == 0), stop=(dc == 7)), reads=[B_Wfg, B_hnT], writes=[B_gp])
            for dc in range(8):
                T.op("pe", M("matmul", out=up[:, 0:256], lhsT=Wfu[:, dc, fb * 128:(fb + 1) * 128], rhs=hnT[:, dc, :],
                                                            start=(dc == 0), stop=(dc == 7)), reads=[B_Wfu, B_hnT], writes=[B_up])
            T.op("act", M("activation", out=sg[:, :], in_=gp[:, 0:256], func=AF.Silu), reads=[B_gp], writes=[B_sg])
            T.op("dve", M("tensor_tensor", out=hidT[:, fb, :], in0=sg[:, :], in1=up[:, 0:256], op=ALU.mult),
                 reads=[B_sg, B_up], writes=[B_hidT])
        for j in range(2):
            hkt, B_hk = hk[j]
            yt, B_yt = yo[j]
            for half in range(2):
                dn, B_dn = PB[4 + half]
                for fb in range(NFB):
                    T.op("pe", M("matmul", out=dn[:, :], lhsT=hidT[:, fb, 128 * j:128 * j + 128],
                                                                rhs=Wfd[:, fb, half * 512:(half + 1) * 512],
                                                                start=(fb == 0), stop=(fb == NFB - 1)),
                         reads=[B_hidT, B_Wfd], writes=[B_dn])
                T.op("dve", M("tensor_tensor", out=hkt[:, half * 512:(half + 1) * 512], in0=dn[:, :],
                                                             in1=hkt[:, half * 512:(half + 1) * 512], op=ALU.add),
                     reads=[B_dn, B_hk], writes=[B_hk])
            T.op("act", M("activation", out=sqj[:, :], in_=hkt[:, :], func=AF.Square, accum_out=stat[:, 4:5]),
                 reads=[B_hk], writes=[B_sqj, B_stat])
            T.op("act", M("activation", out=stat[:, 5:6], in_=stat[:, 4:5], func=AF.Ln, scale=1.0 / D, bias=cf[:, 280:281]),
                 reads=[B_stat, B_cf], writes=[B_stat])
            T.op("act", M("activation", out=stat[:, 6:7], in_=stat[:, 5:6], func=AF.Exp, scale=-0.5), reads=[B_stat], writes=[B_stat])
            T.op("dve", M("scalar_tensor_tensor", out=yt[:, :], in0=hkt[:, :], scalar=stat[:, 6:7], in1=gfin[:, :],
                                                         op0=ALU.mult, op1=ALU.mult), reads=[B_hk, B_stat, B_gfin], writes=[B_yt])
            r0 = 256 * v + 128 * j
            T.dma("sp", M("dma_start", out=y_d[r0:r0 + 128, :], in_=yt[:, :]), B_yt, reads=[B_yt], writes=[B_y])
    T.barrier()

    finish()
    p3.close()
    es.close()
    return nc


_NC = {}


def kernel(x, meta, norm_mix_g, w_in, w_gate, b_gate, lam_q1, lam_k1, lam_q2, lam_k2, subln_g,
           w_br_sb, w_br_df, w_out, norm_ffn_g, w_ffn_gate, w_ffn_up, w_ffn_down, norm_final_g):
    in_maps = prep(x, meta, norm_mix_g, w_in, w_gate, b_gate, lam_q1, lam_k1, lam_q2, lam_k2, subln_g,
                   w_br_sb, w_br_df, w_out, norm_ffn_g, w_ffn_gate, w_ffn_up, w_ffn_down, norm_final_g)
    if "nc" not in _NC:
        _NC["nc"] = build()
    nc = _NC["nc"]
    res = run_bass_kernel_spmd(nc, in_maps, core_ids=list(range(8)))
    out = np.empty((2, S, D), np.float32)
    for c in range(8):
        b, g = c // 4, c % 4
        out[b, 2048 * g:2048 * (g + 1)] = res.results[c]["y"]
    return out


def prep(x, meta, norm_mix_g, w_in, w_gate, b_gate, lam_q1, lam_k1, lam_q2, lam_k2, subln_g,
         w_br_sb, w_br_df, w_out, norm_ffn_g, w_ffn_gate, w_ffn_up, w_ffn_down, norm_final_g):
    f = lambda a: np.ascontiguousarray(np.asarray(a, dtype=np.float32))
    x = f(x)
    w_in = f(w_in)[0]
    lam = np.concatenate([f(lam_q1)[0], f(lam_k1)[0], f(lam_q2)[0], f(lam_k2)[0]])
    lam_b = np.ascontiguousarray(np.broadcast_to(lam[None, :], (128, 256)))
    gfin_b = np.ascontiguousarray(np.broadcast_to(f(norm_final_g)[None, :], (128, D)))
    common = {
        "meta": f(meta), "gmix": _pk(f(norm_mix_g)[0], 8), "wgate": f(w_gate)[0], "bgate": _pk(f(b_gate)[0], 16),
        "wsb": f(w_br_sb)[0], "wdf": f(w_br_df)[0], "wout": f(w_out)[0], "gffn": _pk(f(norm_ffn_g)[0], 8),
        "wfg": f(w_ffn_gate)[0], "wfu": f(w_ffn_up)[0], "wfd": f(w_ffn_down)[0], "gfin": gfin_b, "lam": lam_b,
        "subg": f(subln_g)[0].reshape(128, 1).copy(),
    }
    in_maps = []
    for c in range(8):
        b, g = c // 4, c % 4
        cols = []
        for h in (2 * g, 2 * g + 1):
            cols += [np.arange(h * 128, (h + 1) * 128), np.arange(1024 + h * 128, 1024 + (h + 1) * 128)]
        for h in (2 * g, 2 * g + 1):
            cols += [np.arange(3072 + h * 128, 3072 + (h + 1) * 128), np.arange(4096 + h * 128, 4096 + (h + 1) * 128)]
        for h in (2 * g, 2 * g + 1):
            cols += [np.arange(2048 + h * 128, 2048 + (h + 1) * 128)]
        for h in (2 * g, 2 * g + 1):
            cols += [np.arange(5120 + h * 128, 5120 + (h + 1) * 128)]
        w1 = np.ascontiguousarray(w_in[:, np.concatenate(cols)])
        cbv, c3v, cfv, kbv = _consts(g)
        m = dict(common)
        m.update({"xb": x[b], "x2": np.ascontiguousarray(x[b, 2048 * g:2048 * (g + 1)]), "w1": w1,
                  "cb": cbv, "c3": c3v, "cf": cfv, "kb": kbv})
        in_maps.append(m)
    return in_maps
```

```python
import contextlib
import math

import ml_dtypes
import numpy as np

import concourse.bass as bass
import concourse.mybir as mybir
from concourse.bass_utils import run_bass_kernel_spmd

F32 = mybir.dt.float32
BF = mybir.dt.bfloat16
AF = mybir.ActivationFunctionType
ALU = mybir.AluOpType
AX = mybir.AxisListType

D = 1024
S = 8192
NM = 16
L = S + NM
NST = 16
DFF = 2816
NFB = DFF // 128
EPS = 1e-6
SB_SCALE = 128 ** -0.5
DF_SCALE = 64 ** -0.5
LAM_INIT = 0.8 - 0.6 * math.exp(0.0)
BIGNEG = -30000.0
NCH = 8
SAME_ENG_SYNC = ("act", "dve", "pool")


def M(name, *a, **kw):
    return lambda e: getattr(e, name)(*a, **kw)


class Buf:
    def __init__(self, name):
        self.name = name
        self.w = None
        self.r = []
        self.dsem = None
        self.dcnt = 0


class Eng:
    def __init__(self, name):
        self.name = name
        self.sem = "e_" + name
        self.cnt = 0
        self.q = []


class Tracker:
    def __init__(self):
        self.eng = {n: Eng(n) for n in ("pe", "act", "dve", "pool", "sp")}
        self.semnames = [e.sem for e in self.eng.values()]
        self.bufs = []

    def buf(self, name):
        b = Buf(name)
        self.bufs.append(b)
        return b

    def _collect(self, eng, reads, writes):
        waits = {}

        def add(tok):
            if tok is None:
                return
            sem, val = tok
            if sem == eng.sem and eng.name not in SAME_ENG_SYNC:
                return
            if waits.get(sem, 0) < val:
                waits[sem] = val

        for b in reads:
            add(b.w)
        for b in writes:
            add(b.w)
            for t in b.r:
                add(t)
        return waits

    def _update(self, tok, reads, writes):
        for b in reads:
            b.r.append(tok)
        for b in writes:
            b.w = tok
            b.r = []

    def op(self, en, fn, reads=(), writes=()):
        eng = self.eng[en]
        waits = self._collect(eng, reads, writes)
        eng.cnt += 1
        tok = (eng.sem, eng.cnt)
        eng.q.append((waits, fn, (eng.sem, 1)))
        self._update(tok, reads, writes)
        return tok

    def dma(self, en, fn, sb, reads=(), writes=()):
        eng = self.eng[en]
        if sb.dsem is None:
            sb.dsem = "d_%d_%s" % (len(self.semnames), sb.name)
            self.semnames.append(sb.dsem)
        waits = self._collect(eng, reads, writes)
        sb.dcnt += 16
        tok = (sb.dsem, sb.dcnt)
        eng.q.append((waits, fn, (sb.dsem, 16)))
        self._update(tok, reads, writes)
        return tok

    def cc(self, fn, sem, reads=(), writes=()):
        eng = self.eng["pool"]
        waits = self._collect(eng, reads, writes)
        if sem not in self.semnames:
            self.semnames.append(sem)
        tok = (sem, 1)
        eng.q.append((waits, fn, (sem, 1)))
        self._update(tok, reads, writes)
        return tok

    def barrier(self):
        toks = {}
        for e in self.eng.values():
            if e.cnt:
                toks[e.sem] = e.cnt
        for b in self.bufs:
            for t in [b.w] + b.r:
                if t is not None and toks.get(t[0], 0) < t[1]:
                    toks[t[0]] = t[1]
        for e in self.eng.values():
            w = {s: v for s, v in toks.items() if s != e.sem}
            e.q.append((w, None, None))

    def replay(self, en, e, sems):
        eng = self.eng[en]
        known = {}
        for waits, fn, inc in eng.q:
            for s, v in waits.items():
                if known.get(s, 0) < v:
                    e.wait_ge(sems[s], v)
                    known[s] = v
            if fn is None:
                continue
            ins = fn(e)
            ins.then_inc(sems[inc[0]], inc[1])


def _consts(g):
    bf = ml_dtypes.bfloat16
    i = np.arange(128)
    k = i[:, None]
    q = i[None, :]
    ident = (k == q).astype(np.float32)
    tri_i = (k >= q).astype(np.float32)
    tri_c = (k < q).astype(np.float32)
    negm = np.where(k >= q, BIGNEG, 0.0)
    bigp = np.where(k > q, 100.0, 0.0)
    sh = (k == q - 1).astype(np.float32) - (k == q).astype(np.float32)
    sel = ((k == 127) & (q == 0)).astype(np.float32)
    selm = ((k == 15) & (q == 0)).astype(np.float32)
    cb = np.zeros((128, 2048 + 512), np.float32)
    cb[:, 0:128] = ident
    cb[:, 128:256] = tri_i
    cb[:, 256:384] = tri_c
    cb[:, 384:512] = negm
    cb[:, 512:640] = bigp
    cb[:, 640:768] = -bigp
    cb[:, 768:896] = sh
    cb[:, 896:1024] = sel
    cb[:, 1024:1152] = selm
    c3 = np.zeros((3, 2 * 656), np.float32)
    cf = np.zeros((128, 128 + 2 * 80), np.float32)
    cf[:, 0:128] = 1.0
    for hh in range(2):
        h = 2 * g + hh
        slope = 2.0 ** (-(h + 1))
        c = slope / DF_SCALE
        bc = np.where(k > q, -2.0 * c * (k - q), 0.0)
        bc = np.where((k >= 64) & (q < 64), BIGNEG, bc)
        cb[:, 1152 + 128 * hh:1280 + 128 * hh] = bc
        o = 656 * hh
        c3[0, o:o + 128] = c * i
        c3[1, o:o + 128] = 1.0
        c3[2, o:o + 128] = 1.0
        c3[0, o + 128:o + 144] = c * (np.arange(16) - 16)
        c3[1, o + 128:o + 144] = 1.0
        c3[2, o + 128:o + 144] = 1.0
        j = np.arange(512)
        c3[0, o + 144:o + 656] = 1.0
        c3[1, o + 144:o + 656] = -c * (j % 128)
        c3[2, o + 144:o + 656] = -c * 128 * (j // 128)
        for dl in range(-3, 70):
            cf[:, 128 + 80 * hh + dl + 3] = -slope * 128.0 * dl
    kb = np.zeros((2, 3, L), np.float32)
    pos = np.arange(L)
    ii = np.where(pos < NM, pos - NM, (pos - NM) % 128)
    for hh in range(2):
        h = 2 * g + hh
        c = (2.0 ** (-(h + 1))) / DF_SCALE
        kb[hh, 0] = c * ii
        kb[hh, 1] = 1.0
        kb[hh, 2] = 1.0
    return cb.astype(bf), c3.astype(bf), cf, kb.astype(bf)


def _pk(v, n):
    return np.ascontiguousarray(np.asarray(v, np.float32).reshape(n, 128).T)


def build(stop=None):
    nc = bass.Bass("TRN2", target_bir_lowering=False)
    T = Tracker()

    def din(name, shape, dt=F32):
        return nc.dram_tensor(name, list(shape), dt, kind="ExternalInput").ap()

    xb_d = din("xb", [S, D])
    meta_d = din("meta", [NM, D])
    x2_d = din("x2", [2048, D])
    w1_d = din("w1", [D, 1536])
    gmix_d = din("gmix", [128, 8])
    wgate_d = din("wgate", [D, 2048])
    bgate_d = din("bgate", [128, 16])
    wsb_d = din("wsb", [D, D])
    wdf_d = din("wdf", [D, D])
    wout_d = din("wout", [D, D])
    gffn_d = din("gffn", [128, 8])
    wfg_d = din("wfg", [D, DFF])
    wfu_d = din("wfu", [D, DFF])
    wfd_d = din("wfd", [DFF, D])
    gfin_d = din("gfin", [128, D])
    lam_d = din("lam", [128, 256])
    subg_d = din("subg", [128, 1])
    cb_d = din("cb", [128, 2560], BF)
    c3_d = din("c3", [3, 1312], BF)
    cf_d = din("cf", [128, 288])
    kb_d = din("kb", [2, 3, L], BF)
    y_d = nc.dram_tensor("y", [2048, D], F32, kind="ExternalOutput").ap()
    arb_d = nc.dram_tensor("arb", [NCH * 4096, 1024], BF).ap()
    arg_d = nc.dram_tensor("arg", [NCH * 4096, 1024], BF).ap()
    hbuf_d = nc.dram_tensor("hbuf", [2048, D], F32).ap()
    own_d = nc.dram_tensor("own", [NCH * 512, 1024], BF).ap()
    osel_d = nc.dram_tensor("osel", [2 * 2048, 1024], BF).ap()

    B_arb = [T.buf("arb%d" % c) for c in range(NCH)]
    B_arg = [T.buf("arg%d" % c) for c in range(NCH)]
    B_hbuf = [T.buf("hbuf%d" % c) for c in range(16)]
    B_y = T.buf("y")
    B_own = [T.buf("own%d" % c) for c in range(NCH)]
    B_ownx = [T.buf("ownx%d" % c) for c in range(NCH)]
    B_osel = [T.buf("osel%d" % c) for c in range(2)]

    es = contextlib.ExitStack()

    def sb(name, shape, dt, stack=None):
        t = (stack or es).enter_context(nc.sbuf_tensor("s_" + name, list(shape), dt))
        return t, T.buf(name)

    def ps(name, shape, dt, stack=None):
        t = (stack or es).enter_context(nc.psum_tensor("p_" + name, list(shape), dt))
        return t, T.buf(name)

    pidc = {}

    def PID(e):
        if "p" not in pidc:
            pidc["p"] = e.partition_id()
        return pidc["p"]


    def finish():
        T.barrier()
        with contextlib.ExitStack() as ss:
            sems = {n: ss.enter_context(nc.semaphore(n)) for n in T.semnames}
            block = ss.enter_context(nc.Block())

            @block.tensor
            def _(e):
                T.replay("pe", e, sems)

            @block.scalar
            def _(e):
                T.replay("act", e, sems)

            @block.vector
            def _(e):
                T.replay("dve", e, sems)

            @block.gpsimd
            def _(e):
                T.replay("pool", e, sems)

            @block.sync
            def _(e):
                T.replay("sp", e, sems)

    def dump(name, src, B_src, rows, cols, dt):
        d = nc.dram_tensor(name, [rows, cols], dt, kind="ExternalOutput").ap()
        T.dma("sp", M("dma_start", out=d[:, :], in_=src), T.buf(name + "_s"), reads=[B_src], writes=[T.buf(name)])

    cb, B_cb = sb("cb", [128, 2560], BF)
    c3, B_c3 = sb("c3", [3, 1312], BF)
    cf, B_cf = sb("cf", [128, 288], F32)
    lamt, B_lam = sb("lamt", [128, 256], F32)
    lamw, B_lamw = sb("lamw", [128, 8], F32)
    subg, B_subg = sb("subg", [128, 1], F32)
    gmix, B_gmix = sb("gmix", [128, 8], F32)
    gffn, B_gffn = sb("gffn", [128, 8], F32)
    bgate, B_bgate = sb("bgate", [128, 16], F32)
    stat, B_stat = sb("stat", [128, 8], F32)
    xt = [sb("xt%d" % i, [128, D], F32) for i in range(2)]
    xnb, B_xnb = sb("xnb", [128, D], BF)
    sqj, B_sqj = sb("sqj", [128, D], BF)
    wstg = [sb("wstg%d" % i, [128, 1024], F32) for i in range(2)]
    PB = [ps("pb%d" % i, [128, 512], F32) for i in range(7)]
    tp, B_tp = ps("tp", [128, 8, 128], BF)

    ident = cb[:, 0:128]
    triI = cb[:, 128:256]
    triC = cb[:, 256:384]
    negm = cb[:, 384:512]
    bigp = cb[:, 512:640]
    bigpn = cb[:, 640:768]
    shm = cb[:, 768:896]
    selc = cb[:, 896:1024]
    selmc = cb[:, 1024:1152]
    zer = cb[:, 2048:2560]
    onesf = cf[:, 0:128]

    T.dma("sp", M("dma_start", out=cb[:, :], in_=cb_d[:, :]), B_cb, writes=[B_cb])
    T.dma("sp", M("dma_start", out=c3[:, :], in_=c3_d[:, :]), B_c3, writes=[B_c3])
    T.dma("sp", M("dma_start", out=cf[:, :], in_=cf_d[:, :]), B_cf, writes=[B_cf])
    T.dma("sp", M("dma_start", out=lamt[:, :], in_=lam_d[:, :]), B_lam, writes=[B_lam])
    T.dma("sp", M("dma_start", out=subg[:, :], in_=subg_d[:, :]), B_subg, writes=[B_subg])
    T.dma("sp", M("dma_start", out=gmix[:, :], in_=gmix_d[:, :]), B_gmix, writes=[B_gmix])
    T.dma("sp", M("dma_start", out=gffn[:, :], in_=gffn_d[:, :]), B_gffn, writes=[B_gffn])
    T.dma("sp", M("dma_start", out=bgate[:, :], in_=bgate_d[:, :]), B_bgate, writes=[B_bgate])

    T.op("dve", M("tensor_tensor", out=lamt[:, 0:64], in0=lamt[:, 0:64], in1=lamt[:, 64:128], op=ALU.mult),
         reads=[B_lam], writes=[B_lam])
    T.op("dve", M("tensor_tensor", out=lamt[:, 128:192], in0=lamt[:, 128:192], in1=lamt[:, 192:256], op=ALU.mult),
         reads=[B_lam], writes=[B_lam])
    T.op("dve", M("reduce_sum", out=lamw[:, 0:1], in_=lamt[:, 0:64], axis=AX.X), reads=[B_lam], writes=[B_lamw])
    T.op("dve", M("reduce_sum", out=lamw[:, 1:2], in_=lamt[:, 128:192], axis=AX.X), reads=[B_lam], writes=[B_lamw])
    T.op("act", M("activation", out=lamw[:, 2:4], in_=lamw[:, 0:2], func=AF.Exp), reads=[B_lamw], writes=[B_lamw])
    T.op("dve", M("tensor_tensor", out=lamw[:, 4:5], in0=lamw[:, 3:4], in1=lamw[:, 2:3], op=ALU.subtract),
         reads=[B_lamw], writes=[B_lamw])
    T.op("dve", M("tensor_scalar", out=lamw[:, 4:5], in0=lamw[:, 4:5], scalar1=-LAM_INIT, scalar2=None, op0=ALU.add),
         reads=[B_lamw], writes=[B_lamw])
    T.op("dve", M("tensor_scalar", out=subg[:, 0:1], in0=subg[:, 0:1], scalar1=1.0 - LAM_INIT, scalar2=None, op0=ALU.mult),
         reads=[B_subg], writes=[B_subg])

    wq = {"i": 0}

    def load_w(src, k0, nk, c0, ncol, dst, B_dst, dk0, dc0, gain):
        for kc in range(nk):
            for cc in range(0, ncol, 1024):
                w = min(1024, ncol - cc)
                st, B_st = wstg[wq["i"] % 2]
                wq["i"] += 1
                r0 = (k0 + kc) * 128
                a = c0 + cc
                T.dma("sp", M("dma_start", out=st[:, 0:w], in_=src[r0:r0 + 128, a:a + w]),
                      B_st, writes=[B_st])
                if gain is None:
                    T.op("dve", M("tensor_copy",
                        out=dst[:, dk0 + kc, dc0 + cc:dc0 + cc + w], in_=st[:, 0:w]), reads=[B_st], writes=[B_dst])
                else:
                    gt, B_g = gain
                    T.op("dve", M("tensor_scalar",
                        out=dst[:, dk0 + kc, dc0 + cc:dc0 + cc + w], in0=st[:, 0:w],
                        scalar1=gt[:, k0 + kc:k0 + kc + 1], scalar2=None, op0=ALU.mult),
                        reads=[B_st, B_g], writes=[B_dst])

    xq = {"i": 0}

    def norm_tile(src_fn, rows, dst, B_dst, col0, src_reads=()):
        xtile, B_x = xt[xq["i"] % 2]
        xq["i"] += 1
        src_fn(xtile, B_x)
        T.op("act", M("activation", out=sqj[0:rows, :], in_=xtile[0:rows, :], func=AF.Square,
                                           accum_out=stat[0:rows, 0:1]), reads=[B_x], writes=[B_sqj, B_stat])
        T.op("act", M("activation", out=stat[0:rows, 1:2], in_=stat[0:rows, 0:1], func=AF.Ln,
                                           scale=1.0 / D, bias=cf[0:rows, 280:281]), reads=[B_stat, B_cf], writes=[B_stat])
        T.op("act", M("activation", out=stat[0:rows, 2:3], in_=stat[0:rows, 1:2], func=AF.Exp, scale=-0.5),
             reads=[B_stat], writes=[B_stat])
        T.op("dve", M("tensor_scalar", out=xnb[0:rows, :], in0=xtile[0:rows, :], scalar1=stat[0:rows, 2:3],
                                              scalar2=None, op0=ALU.mult), reads=[B_x, B_stat], writes=[B_xnb])
        for c in range(8):
            T.op("pe", M("transpose", out=tp[:, c, 0:rows], in_=xnb[0:rows, c * 128:(c + 1) * 128],
                                                 identity=ident[0:rows, 0:rows]), reads=[B_xnb, B_cb], writes=[B_tp])
        T.op("dve", M("tensor_copy", out=dst[:, :, col0:col0 + rows], in_=tp[:, :, 0:rows]),
             reads=[B_tp], writes=[B_dst])
        return xtile, B_x

    p1 = contextlib.ExitStack()
    GRPS = list(range(-1, NST))
    wb, B_wb = sb("wb", [128, 8, 768], BF, p1)
    KT4 = [sb("KT%d" % a, [128, L], BF, p1)[0] for a in range(4)]
    KTB = [{g_: T.buf("KT%d_%d" % (a, g_)) for g_ in GRPS} for a in range(4)]
    TMt = [sb("TM%d" % h, [128, 65, 128], BF, p1)[0] for h in range(2)]
    TMB = [{g_: T.buf("TM%d_%d" % (h, g_)) for g_ in GRPS} for h in range(2)]
    QT = [[sb("QT%d_%d" % (h, par), [128, 512], BF, p1) for par in range(2)] for h in range(2)]
    VT = [[sb("VT%d_%d" % (h, par), [128, 512], BF, p1) for par in range(2)] for h in range(2)]
    QTd = [[[sb("QTd%d_%d_%d" % (h, c, par), [128, 512], BF, p1) for par in range(2)] for c in range(2)] for h in range(2)]
    xnT, B_xnT = sb("xnT", [128, 8, 512], BF, p1)
    vtok = [sb("vtok%d" % i, [128, 256], BF, p1) for i in range(2)]
    Ub = [sb("U%d" % h, [128, 512], BF, p1) for h in range(2)]
    Pb = [sb("P%d" % h, [128, 512], BF, p1) for h in range(2)]
    Eb = [[sb("E%d_%d" % (par, c), [128, 512], BF, p1) for c in range(2)] for par in range(2)]
    Eacc1 = [[sb("Eacc%d_%d" % (h, c), [128, 512], F32, p1) for c in range(2)] for h in range(2)]
    Eacc = [Eacc1, Eacc1]
    odsb = [[sb("od%d_%d" % (h, c), [128, 512], F32, p1) for c in range(2)] for h in range(2)]
    osb = [sb("osb%d" % h, [128, 512], BF, p1) for h in range(2)]
    t0, B_t0 = sb("t0", [128, 512], F32, p1)
    t1, B_t1 = sb("t1", [128, 512], F32, p1)
    t2, B_t2 = sb("t2", [128, 512], F32, p1)
    zt, B_zt = sb("zt", [128, 1024], BF, p1)
    wcv = [sb("wcv%d" % i, [128, 1024], BF, p1) for i in range(2)]

    def grp_of_kt(kt):
        return -1 if kt is None else kt // 4

    T.op("dve", M("memset", cf[:, 280:281], EPS), reads=[B_cf], writes=[B_cf])
    T.op("dve", M("memset", zt[:, :], 0.0), writes=[B_zt])
    if stop is not None:
        for h in range(2):
            T.op("dve", M("memset", TMt[h][:, 0, :], 0.0), writes=[TMB[h][-1]])

    ztok = None
    for c in range(NCH if stop is None else 0):
        for blk in range(32):
            r0 = c * 4096 + blk * 128
            ztok = T.dma("pool", M("dma_start", out=arb_d[r0:r0 + 128, :], in_=zt[:, :]), B_zt,
                         reads=[B_zt])
    for c in range(NCH if stop is None else 0):
        B_arb[c].w = ztok

    bg = []

    def pump(k=1):
        for _ in range(k):
            for g_ in list(bg):
                try:
                    next(g_)
                except StopIteration:
                    bg.remove(g_)

    def drain(g_):
        for _ in g_:
            pass
        if g_ in bg:
            bg.remove(g_)

    def norm_gen(src_fn, rows, dst, B_dst, col0):
        xtile, B_x = xt[xq["i"] % 2]
        xq["i"] += 1
        src_fn(xtile, B_x)
        yield
        T.op("act", M("activation", out=sqj[0:rows, :], in_=xtile[0:rows, :], func=AF.Square,
                      accum_out=stat[0:rows, 0:1]), reads=[B_x], writes=[B_sqj, B_stat])
        T.op("act", M("activation", out=stat[0:rows, 1:2], in_=stat[0:rows, 0:1], func=AF.Ln,
                      scale=1.0 / D, bias=cf[0:rows, 280:281]), reads=[B_stat, B_cf], writes=[B_stat])
        T.op("act", M("activation", out=stat[0:rows, 2:3], in_=stat[0:rows, 1:2], func=AF.Exp, scale=-0.5),
             reads=[B_stat], writes=[B_stat])
        yield
        T.op("dve", M("tensor_scalar", out=xnb[0:rows, :], in0=xtile[0:rows, :], scalar1=stat[0:rows, 2:3],
                      scalar2=None, op0=ALU.mult), reads=[B_x, B_stat], writes=[B_xnb])
        yield
        for c in range(8):
            T.op("pe", M("transpose", out=tp[:, c, 0:rows], in_=xnb[0:rows, c * 128:(c + 1) * 128],
                         identity=ident[0:rows, 0:rows]), reads=[B_xnb, B_cb], writes=[B_tp])
        yield
        T.op("dve", M("tensor_copy", out=dst[:, :, col0:col0 + rows], in_=tp[:, :, 0:rows]),
             reads=[B_tp], writes=[B_dst])
        yield

    def front_gen(pass_id, grp):
        if grp < 0:
            tiles = [(16, 0)]
            n = 16
            pos0 = 0
        else:
            tiles = [(128, j * 128) for j in range(4)]
            n = 512
            pos0 = NM + 512 * grp
        par = grp % 2
        for j, (rows, col0) in enumerate(tiles):
            if grp < 0:
                src = lambda xtile, B_x: T.dma("sp", M("dma_start", out=xtile[0:16, :], in_=meta_d[:, :]), B_x, writes=[B_x])
            else:
                r0 = 512 * grp + 128 * j
                src = lambda xtile, B_x, r0=r0: T.dma("sp", M("dma_start", out=xtile[:, :], in_=xb_d[r0:r0 + 128, :]), B_x, writes=[B_x])
            yield from norm_gen(src, rows, xnT, B_xnT, col0)
        pj, B_pj = PB[6]
        for h in range(2):
            for kind in range(2 if pass_id == 1 else 0):
                for c in range(2):
                    c0 = (2 * h + kind) * 128 + 64 * c
                    if kind == 0 and grp < 0:
                        continue
                    for dc in range(8):
                        T.op("pe", M("matmul", out=pj[0:64, 0:n], lhsT=wb[:, dc, c0:c0 + 64], rhs=xnT[:, dc, 0:n],
                                     start=(dc == 0), stop=(dc == 7)), reads=[B_wb, B_xnT], writes=[B_pj])
                    yield
                    if kind == 0:
                        qd, B_qd = QTd[h][c][par]
                        T.op("dve", M("tensor_copy", out=qd[0:64, 0:n], in_=pj[0:64, 0:n]), reads=[B_pj], writes=[B_qd])
                    else:
                        T.op("dve", M("tensor_copy", out=KT4[2 * h + c][0:64, pos0:pos0 + n], in_=pj[0:64, 0:n]),
                             reads=[B_pj], writes=[KTB[2 * h + c][grp]])
                    yield
            for kind in range(2 if pass_id == 0 else 0):
                c0 = (2 * h + kind) * 128
                if kind == 0 and grp < 0:
                    continue
                for dc in range(8):
                    T.op("pe", M("matmul", out=pj[:, 0:n], lhsT=wb[:, dc, c0:c0 + 128], rhs=xnT[:, dc, 0:n],
                                 start=(dc == 0), stop=(dc == 7)), reads=[B_wb, B_xnT], writes=[B_pj])
                yield
                if kind == 0:
                    qq, B_qq = QT[h][par]
                    T.op("dve", M("tensor_copy", out=qq[:, 0:n], in_=pj[:, 0:n]), reads=[B_pj], writes=[B_qq])
                else:
                    T.op("dve", M("tensor_copy", out=KT4[h][:, pos0:pos0 + n], in_=pj[:, 0:n]), reads=[B_pj], writes=[KTB[h][grp]])
                yield
            if pass_id == 0 and grp >= 0:
                c0 = 512 + h * 128
                for dc in range(8):
                    T.op("pe", M("matmul", out=pj[:, 0:n], lhsT=wb[:, dc, c0:c0 + 128], rhs=xnT[:, dc, 0:n],
                                 start=(dc == 0), stop=(dc == 7)), reads=[B_wb, B_xnT], writes=[B_pj])
                yield
                vv, B_vv = VT[h][par]
                T.op("dve", M("tensor_copy", out=vv[:, 0:n], in_=pj[:, 0:n]), reads=[B_pj], writes=[B_vv])
                yield
        for j, (rows, col0) in enumerate(tiles):
            ti = 0 if grp < 0 else 1 + 4 * grp + j
            for dc in range(8):
                T.op("pe", M("matmul", out=pj[0:rows, 0:256], lhsT=xnT[:, dc, col0:col0 + rows], rhs=wb[:, dc, 512:768],
                             start=(dc == 0), stop=(dc == 7)), reads=[B_wb, B_xnT], writes=[B_pj])
            yield
            if pass_id == 1:
                for h in range(2):
                    T.op("dve", M("tensor_copy", out=TMt[h][0:rows, ti, :], in_=pj[0:rows, h * 128:(h + 1) * 128]),
                         reads=[B_pj], writes=[TMB[h][grp]])
                yield
            else:
                vc, B_vc = vtok[ti % 2]
                vp, B_vp = vtok[(ti + 1) % 2]
                T.op("dve", M("tensor_copy", out=vc[0:rows, :], in_=pj[0:rows, 0:256]), reads=[B_pj], writes=[B_vc])
                yield
                T.op("pe", M("matmul", out=pj[0:rows, 256:512], lhsT=shm[0:rows, 0:rows], rhs=vc[0:rows, :],
                             start=True, stop=True), reads=[B_vc, B_cb], writes=[B_pj])
                if ti == 1:
                    T.op("pe", M("matmul", out=pj[:, 256:512], lhsT=selmc[0:16, :], rhs=vp[0:16, :], start=False, stop=True,
                                 skip_group_check=True), reads=[B_vp, B_cb], writes=[B_pj])
                elif ti > 1:
                    T.op("pe", M("matmul", out=pj[:, 256:512], lhsT=selc[:, :], rhs=vp[:, :], start=False, stop=True,
                                 skip_group_check=True), reads=[B_vp, B_cb], writes=[B_pj])
                yield
                for h in range(2):
                    T.op("dve", M("tensor_copy", out=TMt[h][0:rows, ti, :], in_=pj[0:rows, 256 + h * 128:256 + (h + 1) * 128]),
                         reads=[B_pj], writes=[TMB[h][grp]])
                yield

    def key_tiles(s):
        lst = []
        for kt in range(4 * s + 3, -1, -1):
            a = kt - 4 * s
            lst.append((128, NM + 128 * kt, 1 + kt, 128 * a if a >= 0 else 0, a >= 0, kt))
        lst.append((16, 0, 0, 0, False, None))
        return lst

    def store_o(s, row_in_slot, src, B_src):
        c = s // 2
        col = (s % 2) * 512
        r0 = c * 512 + row_in_slot
        T.dma("pool", M("dma_start", out=own_d[r0:r0 + 128, col:col + 512], in_=src[:, :]),
              B_src, reads=[B_src], writes=[B_own[c]])

    WB = {}

    def wconv_gen(name, src, K, N, gain):
        dst = nc.dram_tensor(name, [K, N], BF).ap()
        B_d = T.buf(name)
        WB[name] = (dst, B_d)
        i = 0
        for kc in range(K // 128):
            for cc in range(0, N, 1024):
                w = min(1024, N - cc)
                st, B_st = wstg[wq["i"] % 2]
                cv, B_cv = wcv[wq["i"] % 2]
                wq["i"] += 1
                r0 = kc * 128
                T.dma("sp", M("dma_start", out=st[:, 0:w], in_=src[r0:r0 + 128, cc:cc + w]), B_st, writes=[B_st])
                yield
                if gain is None:
                    T.op("dve", M("tensor_copy", out=cv[:, 0:w], in_=st[:, 0:w]), reads=[B_st], writes=[B_cv])
                else:
                    gt, B_g = gain
                    T.op("dve", M("tensor_scalar", out=cv[:, 0:w], in0=st[:, 0:w], scalar1=gt[:, kc:kc + 1], scalar2=None,
                                  op0=ALU.mult), reads=[B_st, B_g], writes=[B_cv])
                yield
                T.dma("sp", M("dma_start", out=dst[r0:r0 + 128, cc:cc + w], in_=cv[:, 0:w]), B_cv, reads=[B_cv], writes=[B_d])
                yield

    def wconv_all():
        yield from wconv_gen("wsb_bf", wsb_d, D, D, None)
        yield from wconv_gen("wdf_bf", wdf_d, D, D, None)
        yield from wconv_gen("wg_bf", wgate_d, D, 2048, (gmix, B_gmix))
        yield from wconv_gen("wo_bf", wout_d, D, D, None)
        yield from wconv_gen("wfg_bf", wfg_d, D, DFF, (gffn, B_gffn))
        yield from wconv_gen("wfu_bf", wfu_d, D, DFF, (gffn, B_gffn))
        yield from wconv_gen("wfd_bf", wfd_d, DFF, D, None)

    load_w(w1_d, 0, 8, 0, 512, wb, B_wb, 0, 0, (gmix, B_gmix))
    load_w(w1_d, 0, 8, 1024, 256, wb, B_wb, 0, 512, (gmix, B_gmix))

    def sb_attention(s, rate):
        tiles = key_tiles(s)
        par = s % 2
        zS = [PB[0], PB[1]]
        Rb = [PB[2], PB[3]]
        oS = [PB[4], PB[5]]
        for h in range(2):
            T.op("pe", M("matmul", out=Rb[h][0][:, :], lhsT=zer[:, 0:128], rhs=zer[:, :], start=True, stop=True),
                 reads=[B_cb], writes=[Rb[h][1]])
            T.op("pe", M("matmul", out=oS[h][0][:, :], lhsT=zer[:, 0:128], rhs=zer[:, :], start=True, stop=True),
                 reads=[B_cb], writes=[oS[h][1]])

        def qk(n, h):
            ksz, kc0, ti, q0, diag, kt = tiles[n]
            z, B_z = zS[h]
            qq, B_qq = QT[h][par]
            T.op("pe", M("matmul", out=z[0:ksz, q0:512], lhsT=KT4[h][:, kc0:kc0 + ksz], rhs=qq[:, q0:512],
                         start=True, stop=True), reads=[KTB[h][grp_of_kt(kt)], B_qq], writes=[B_z])
            if diag:
                T.op("pe", M("matmul", out=z[:, q0:q0 + 128], lhsT=ident, rhs=negm, start=False, stop=True, skip_group_check=True),
                     reads=[B_cb], writes=[B_z])

        for h in range(2):
            qk(0, h)
        for n in range(len(tiles)):
            ksz, kc0, ti, q0, diag, kt = tiles[n]
            last = n == len(tiles) - 1
            for h in range(2):
                z, B_z = zS[h]
                U, B_U = Ub[h]
                R, B_R = Rb[h]
                T.op("act", M("activation", out=z[0:ksz, q0:512], in_=z[0:ksz, q0:512], func=AF.Exp, scale=SB_SCALE),
                     reads=[B_z], writes=[B_z])
                T.op("act", M("activation", out=U[0:ksz, q0:512], in_=z[0:ksz, q0:512], func=AF.Ln, bias=1.0),
                     reads=[B_z], writes=[B_U])
                T.op("pe", M("matmul", out=R[0:ksz, q0:512], lhsT=triI[0:ksz, 0:ksz], rhs=U[0:ksz, q0:512],
                             start=False, stop=True, skip_group_check=True), reads=[B_U, B_cb], writes=[B_R])
                if diag:
                    T.op("pe", M("matmul", out=R[:, q0:q0 + 128], lhsT=ident, rhs=bigp, start=False, stop=True, skip_group_check=True),
                         reads=[B_cb], writes=[B_R])
                if not last:
                    qk(n + 1, h)
            pump(rate)
            for h in range(2):
                U, B_U = Ub[h]
                R, B_R = Rb[h]
                P, B_P = Pb[h]
                o, B_o = oS[h]
                T.op("act", M("activation", out=P[0:ksz, q0:512], in_=R[0:ksz, q0:512], func=AF.Exp, scale=-1.0),
                     reads=[B_R], writes=[B_P])
                if not last:
                    T.op("pe", M("matmul", out=R[:, q0:512], lhsT=triC[:, :], rhs=U[:, q0:512], start=False, stop=True,
                                 skip_group_check=True), reads=[B_U, B_cb], writes=[B_R])
                    if diag:
                        T.op("pe", M("matmul", out=R[:, q0:q0 + 128], lhsT=ident, rhs=bigpn, start=False, stop=True,
                                     skip_group_check=True), reads=[B_cb], writes=[B_R])
                T.op("pe", M("matmul", out=o[:, q0:512], lhsT=TMt[h][0:ksz, ti, :], rhs=P[0:ksz, q0:512],
                             start=False, stop=True, skip_group_check=True), reads=[TMB[h][grp_of_kt(kt)], B_P], writes=[B_o])
        for h in range(2):
            o, B_o = oS[h]
            ob, B_ob = osb[h]
            vv, B_vv = VT[h][par]
            T.op("dve", M("tensor_tensor", out=ob[:, :], in0=o[:, :], in1=vv[:, :], op=ALU.add),
                 reads=[B_o, B_vv], writes=[B_ob])
            store_o(s, h * 128, ob, B_ob)

    NFRONT = 64
    inter = stop is None or stop in ("sb1", "df1")
    stop_s = int(stop[2:]) if stop is not None and stop[:2] in ("sb", "df") else None
    sA = stop is not None and stop[:2] == "df"

    if not sA:
        drain(front_gen(0, -1))
        drain(front_gen(0, 0))
        if inter:
            bg.append(wconv_all())
    for s in range(NST if not sA else 0):
        if stop == "front":
            dump("dbg_kt", KT4[0][:, 0:528], KTB[0][0], 128, 528, BF)
            dump("dbg_tm", TMt[0][:, 0:5, :], TMB[0][0], 128, 640, BF)
            dump("dbg_qt", QT[0][0][0][:, :], QT[0][0][1], 128, 512, BF)
            dump("dbg_vt", VT[0][0][0][:, :], VT[0][0][1], 128, 512, F32)
            finish()
            return nc
        fg = None
        if s + 1 < NST and inter:
            fg = front_gen(0, s + 1)
            bg.insert(0, fg)
        nsteps = 4 * s + 5
        sb_attention(s, -(-NFRONT // nsteps))
        if fg is not None:
            drain(fg)
        if stop is not None and stop[:2] == "sb" and s == stop_s:
            dump("dbg_o0", osb[0][0][:, :], osb[0][1], 128, 512, BF)
            dump("dbg_o1", osb[1][0][:, :], osb[1][1], 128, 512, BF)
            finish()
            return nc
    for g_ in list(bg):
        drain(g_)

    load_w(w1_d, 0, 8, 512, 512, wb, B_wb, 0, 0, (gmix, B_gmix))
    load_w(w1_d, 0, 8, 1280, 256, wb, B_wb, 0, 512, (gmix, B_gmix))

    def df_attention(s, rate, prev_eg=None):
        tiles = key_tiles(s)
        par = s % 2
        zD = [PB[0], PB[1]]
        oD = [[PB[2], PB[3]], [PB[4], PB[5]]]
        for h in range(2):
            for c in range(2):
                T.op("pe", M("matmul", out=oD[h][c][0][:, :], lhsT=zer[:, 0:128], rhs=zer[:, :], start=True, stop=True),
                     reads=[B_cb], writes=[oD[h][c][1]])
                T.op("dve", M("memset", Eacc[par][h][c][0][:, :], 0.0), writes=[Eacc[par][h][c][1]])
        steps = [(n, h) for n in range(len(tiles)) for h in range(2)]

        def qk(m):
            n, h = steps[m]
            ksz, kc0, ti, q0, diag, kt = tiles[n]
            for c in range(2):
                z, B_z = zD[c]
                qd, B_qd = QTd[h][c][par]
                T.op("pe", M("matmul", out=z[0:ksz, q0:512], lhsT=KT4[2 * h + c][:, kc0:kc0 + ksz],
                             rhs=qd[:, q0:512], start=True, stop=True),
                     reads=[KTB[2 * h + c][grp_of_kt(kt)], B_qd], writes=[B_z])
                if diag:
                    T.op("pe", M("matmul", out=z[:, q0:q0 + 128], lhsT=ident, rhs=cb[:, 1152 + 128 * h:1280 + 128 * h],
                                 start=False, stop=True, skip_group_check=True), reads=[B_cb], writes=[B_z])

        def av(m):
            n, h = steps[m]
            ksz, kc0, ti, q0, diag, kt = tiles[n]
            for c in range(2):
                E, B_E = Eb[m % 2][c]
                o, B_o = oD[h][c]
                T.op("pe", M("matmul", out=o[:, q0:512], lhsT=TMt[h][0:ksz, ti, :], rhs=E[0:ksz, q0:512],
                             start=False, stop=True, skip_group_check=True), reads=[TMB[h][grp_of_kt(kt)], B_E], writes=[B_o])

        qk(0)
        for m in range(len(steps)):
            n, h = steps[m]
            ksz, kc0, ti, q0, diag, kt = tiles[n]
            dl = (4 * s - kt) if kt is not None else 4 * s
            kcol = 128 + 80 * h + dl + 3
            for c in range(2):
                z, B_z = zD[c]
                E, B_E = Eb[m % 2][c]
                T.op("act", M("activation", out=E[0:ksz, q0:512], in_=z[0:ksz, q0:512], func=AF.Exp,
                              scale=DF_SCALE, bias=cf[0:ksz, kcol:kcol + 1]), reads=[B_z, B_cf], writes=[B_E])
                ea, B_ea = Eacc[par][h][c]
                T.op("dve", M("tensor_tensor", out=ea[0:ksz, q0:512], in0=ea[0:ksz, q0:512], in1=E[0:ksz, q0:512], op=ALU.add),
                     reads=[B_E, B_ea], writes=[B_ea])
            if m + 1 < len(steps):
                qk(m + 1)
            av(m)
            if h == 1:
                pump(rate)
        if prev_eg is not None:
            drain(prev_eg)
        sm, B_sm = PB[6]
        for h in range(2):
            for c in range(2):
                ea, B_ea = Eacc[par][h][c]
                od, B_od = odsb[h][c]
                T.op("pe", M("matmul", out=sm[:, :], lhsT=onesf, rhs=ea[:, :], start=True, stop=True),
                     reads=[B_ea, B_cf], writes=[B_sm])
                T.op("dve", M("reciprocal", out=od[:, :], in_=sm[:, :]), reads=[B_sm], writes=[B_od])
                T.op("dve", M("tensor_tensor", out=od[:, :], in0=oD[h][c][0][:, :], in1=od[:, :], op=ALU.mult),
                     reads=[oD[h][c][1], B_od], writes=[B_od])

    def df_epilogue(s):
        sm, B_sm = PB[6]
        for h in range(2):
            T.op("dve", M("scalar_tensor_tensor", out=t0[:, :], in0=odsb[h][1][0][:, :], scalar=lamw[:, 4:5], in1=odsb[h][0][0][:, :],
                          op0=ALU.mult, op1=ALU.add), reads=[odsb[h][0][1], odsb[h][1][1], B_lamw], writes=[B_t0])
            T.op("dve", M("tensor_tensor", out=t1[:, :], in0=t0[:, :], in1=t0[:, :], op=ALU.mult), reads=[B_t0], writes=[B_t1])
            yield
            T.op("pe", M("matmul", out=sm[:, :], lhsT=onesf, rhs=t1[:, :], start=True, stop=True),
                 reads=[B_t1, B_cf], writes=[B_sm])
            yield
            T.op("act", M("activation", out=t2[:, :], in_=sm[:, :], func=AF.Ln, scale=1.0 / 128.0, bias=cf[:, 280:281]),
                 reads=[B_sm, B_cf], writes=[B_t2])
            T.op("act", M("activation", out=t2[:, :], in_=t2[:, :], func=AF.Exp, scale=-0.5), reads=[B_t2], writes=[B_t2])
            yield
            T.op("dve", M("tensor_tensor", out=t2[:, :], in0=t2[:, :], in1=t0[:, :], op=ALU.mult), reads=[B_t0, B_t2], writes=[B_t2])
            ob, B_ob = osb[h]
            T.op("dve", M("tensor_scalar", out=ob[:, :], in0=t2[:, :], scalar1=subg[:, 0:1], scalar2=None, op0=ALU.mult),
                 reads=[B_t2, B_subg], writes=[B_ob])
            store_o(s, 256 + h * 128, ob, B_ob)
            yield
        if s % 2 == 1 and stop is None:
            c = s // 2
            T.dma("pool", lambda e, c=c: e.dma_start(
                out=arb_d[bass.ds(PID(e) * 512 + c * 4096, 512), :], in_=own_d[c * 512:(c + 1) * 512, :]),
                B_ownx[c], reads=[B_own[c]], writes=[B_arb[c]])
            T.cc(M("collective_compute", "AllReduce", ALU.add, replica_groups=[list(range(8))],
                   ins=[arb_d[c * 4096:(c + 1) * 4096, :]], outs=[arg_d[c * 4096:(c + 1) * 4096, :]]),
                 "cc%d" % c, reads=[B_arb[c]], writes=[B_arg[c]])

    allg = [g_ for g_ in GRPS]
    for h in range(2):
        for c in range(2):
            T.op("dve", M("memset", KT4[2 * h + c][64:128, :], 0.0), writes=[KTB[2 * h + c][g_] for g_ in allg])
            T.dma("sp", M("dma_start", out=KT4[2 * h + c][64:67, :], in_=kb_d[h, :, :]), KTB[2 * h + c][-1],
                  writes=[KTB[2 * h + c][g_] for g_ in allg])
            for par in range(2):
                qd, B_qd = QTd[h][c][par]
                T.op("dve", M("memset", qd[64:128, :], 0.0), writes=[B_qd])
                T.dma("sp", M("dma_start", out=qd[64:67, :], in_=c3_d[:, 656 * h + 144:656 * h + 656]), B_qd, writes=[B_qd])
    drain(front_gen(1, -1))
    drain(front_gen(1, 0))
    prev_eg = None
    for s in range(NST):
        fg = None
        if s + 1 < NST and inter:
            fg = front_gen(1, s + 1)
            bg.insert(0, fg)
        nsteps = 4 * s + 5
        df_attention(s, -(-NFRONT // nsteps), prev_eg)
        if fg is not None:
            drain(fg)
        eg = df_epilogue(s)
        prev_eg = eg
        if sA and s == stop_s:
            drain(eg)
            dump("dbg_o0", osb[0][0][:, :], osb[0][1], 128, 512, BF)
            dump("dbg_o1", osb[1][0][:, :], osb[1][1], 128, 512, BF)
            finish()
            return nc
        bg.append(eg)
    for g_ in list(bg):
        drain(g_)

    T.barrier()
    p1.close()

    p2 = contextlib.ExitStack()
    Wsb, B_Wsb = sb("Wsb", [128, 8, 1024], BF, p2)
    Wdf, B_Wdf = sb("Wdf", [128, 8, 1024], BF, p2)
    Wg, B_Wg = sb("Wg", [128, 8, 2048], BF, p2)
    Wo, B_Wo = sb("Wo", [128, 8, 1024], BF, p2)
    oT, B_oT = sb("oT", [128, 16, 512], BF, p2)
    xnT2, B_xnT2 = sb("xnT2", [128, 8, 512], BF, p2)
    mgT, B_mgT = sb("mgT", [128, 8, 512], BF, p2)
    xk = [sb("xk%d" % j, [128, D], F32, p2) for j in range(4)]
    g0, B_g0 = sb("g0", [128, 512], F32, p2)
    g1, B_g1 = sb("g1", [128, 512], F32, p2)
    ht = [sb("ht%d" % j, [128, D], F32, p2) for j in range(2)]
    for kc_ in range(8):
        T.dma("sp", M("dma_start", out=Wsb[:, kc_, :], in_=WB["wsb_bf"][0][kc_ * 128:(kc_ + 1) * 128, :]), B_Wsb,
              reads=[WB["wsb_bf"][1]], writes=[B_Wsb])
    for kc_ in range(8):
        T.dma("sp", M("dma_start", out=Wdf[:, kc_, :], in_=WB["wdf_bf"][0][kc_ * 128:(kc_ + 1) * 128, :]), B_Wdf,
              reads=[WB["wdf_bf"][1]], writes=[B_Wdf])
    for kc_ in range(8):
        T.dma("sp", M("dma_start", out=Wg[:, kc_, :], in_=WB["wg_bf"][0][kc_ * 128:(kc_ + 1) * 128, :]), B_Wg,
              reads=[WB["wg_bf"][1]], writes=[B_Wg])
    for kc_ in range(8):
        T.dma("sp", M("dma_start", out=Wo[:, kc_, :], in_=WB["wo_bf"][0][kc_ * 128:(kc_ + 1) * 128, :]), B_Wo,
              reads=[WB["wo_bf"][1]], writes=[B_Wo])

    for u in range(4):
        for j in range(4):
            xkt, B_xk = xk[j]
            r0 = 512 * u + 128 * j

            def src(xtile, B_x, r0=r0, xkt=xkt, B_xk=B_xk):
                T.dma("sp", M("dma_start", out=xtile[:, :], in_=x2_d[r0:r0 + 128, :]), B_x, writes=[B_x])
                T.dma("sp", M("dma_start", out=xkt[:, :], in_=x2_d[r0:r0 + 128, :]), B_xk, writes=[B_xk])
            norm_tile(src, 128, xnT2, B_xnT2, 128 * j)
        col = (u % 2) * 512
        cc2 = u // 2
        if u % 2 == 0:
            T.dma("pool", lambda e, cc2=cc2: e.dma_start(
                out=osel_d[cc2 * 2048:(cc2 + 1) * 2048, :],
                in_=arg_d[bass.ds(((PID(e) % 4) * 2 + cc2) * 4096 + (PID(e) // 4) * 2048, 2048), :]),
                B_osel[cc2], reads=[B_arg[2 * i_ + cc2] for i_ in range(4)], writes=[B_osel[cc2]])
        T.dma("sp", M("dma_start",
            out=oT[:, :, :],
            in_=osel_d[cc2 * 2048:(cc2 + 1) * 2048, col:col + 512].rearrange("(k p) n -> p k n", p=128)),
            B_oT, reads=[B_osel[cc2]], writes=[B_oT])
        for blk in range(8):
            ysb, B_ysb = PB[0]
            ydf, B_ydf = PB[1]
            ga, B_ga = PB[2]
            gb, B_gb = PB[3]
            idx = 0
            for r in range(4):
                for jj in range(2):
                    hd = 2 * r + jj
                    T.op("pe", M("matmul",
                        out=ysb[:, :], lhsT=Wsb[:, hd, blk * 128:(blk + 1) * 128], rhs=oT[:, 4 * r + jj, :],
                        start=(idx == 0), stop=(idx == 7)), reads=[B_Wsb, B_oT], writes=[B_ysb])
                    idx += 1
            idx = 0
            for r in range(4):
                for jj in range(2):
                    hd = 2 * r + jj
                    T.op("pe", M("matmul",
                        out=ydf[:, :], lhsT=Wdf[:, hd, blk * 128:(blk + 1) * 128], rhs=oT[:, 4 * r + 2 + jj, :],
                        start=(idx == 0), stop=(idx == 7)), reads=[B_Wdf, B_oT], writes=[B_ydf])
                    idx += 1
            for dc in range(8):
                T.op("pe", M("matmul", out=ga[:, :], lhsT=Wg[:, dc, blk * 128:(blk + 1) * 128], rhs=xnT2[:, dc, :],
                                                     start=(dc == 0), stop=(dc == 7)), reads=[B_Wg, B_xnT2], writes=[B_ga])
            for dc in range(8):
                T.op("pe", M("matmul", out=gb[:, :], lhsT=Wg[:, dc, 1024 + blk * 128:1024 + (blk + 1) * 128],
                                                     rhs=xnT2[:, dc, :], start=(dc == 0), stop=(dc == 7)),
                     reads=[B_Wg, B_xnT2], writes=[B_gb])
            T.op("act", M("activation", out=g0[:, :], in_=ga[:, :], func=AF.Sigmoid, bias=bgate[:, blk:blk + 1]),
                 reads=[B_ga, B_bgate], writes=[B_g0])
            T.op("act", M("activation", out=g1[:, :], in_=gb[:, :], func=AF.Sigmoid, bias=bgate[:, 8 + blk:9 + blk]),
                 reads=[B_gb, B_bgate], writes=[B_g1])
            T.op("dve", M("tensor_tensor", out=g0[:, :], in0=g0[:, :], in1=ysb[:, :], op=ALU.mult), reads=[B_g0, B_ysb], writes=[B_g0])
            T.op("dve", M("tensor_tensor", out=g1[:, :], in0=g1[:, :], in1=ydf[:, :], op=ALU.mult), reads=[B_g1, B_ydf], writes=[B_g1])
            T.op("dve", M("tensor_tensor", out=mgT[:, blk, :], in0=g0[:, :], in1=g1[:, :], op=ALU.add),
                 reads=[B_g0, B_g1], writes=[B_mgT])
        for j in range(4):
            h_t, B_h = ht[j % 2]
            xkt, B_xk = xk[j]
            for half in range(2):
                mo, B_mo = PB[4 + half]
                for blk in range(8):
                    T.op("pe", M("matmul", out=mo[:, :], lhsT=mgT[:, blk, 128 * j:128 * j + 128],
                                                           rhs=Wo[:, blk, half * 512:(half + 1) * 512], start=(blk == 0), stop=(blk == 7)),
                         reads=[B_mgT, B_Wo], writes=[B_mo])
                T.op("dve", M("tensor_tensor", out=h_t[:, half * 512:(half + 1) * 512], in0=mo[:, :],
                                                             in1=xkt[:, half * 512:(half + 1) * 512], op=ALU.add),
                     reads=[B_mo, B_xk], writes=[B_h])
            r0 = 512 * u + 128 * j
            T.dma("sp", M("dma_start", out=hbuf_d[r0:r0 + 128, :], in_=h_t[:, :]), B_h,
                  reads=[B_h], writes=[B_hbuf[4 * u + j]])
    T.barrier()
    p2.close()

    p3 = contextlib.ExitStack()
    Wfg, B_Wfg = sb("Wfg", [128, 8, DFF], BF, p3)
    Wfu, B_Wfu = sb("Wfu", [128, 8, DFF], BF, p3)
    Wfd, B_Wfd = sb("Wfd", [128, NFB, 1024], BF, p3)
    gfin, B_gfin = sb("gfin", [128, D], F32, p3)
    hnT, B_hnT = sb("hnT", [128, 8, 256], BF, p3)
    hidT, B_hidT = sb("hidT", [128, NFB, 256], BF, p3)
    hk = [sb("hk%d" % j, [128, D], F32, p3) for j in range(2)]
    sg, B_sg = sb("sg", [128, 256], F32, p3)
    yo = [sb("yo%d" % j, [128, D], F32, p3) for j in range(2)]
    T.dma("sp", M("dma_start", out=gfin[:, :], in_=gfin_d[:, :]), B_gfin, writes=[B_gfin])
    for kc_ in range(8):
        T.dma("sp", M("dma_start", out=Wfg[:, kc_, :], in_=WB["wfg_bf"][0][kc_ * 128:(kc_ + 1) * 128, :]), B_Wfg,
              reads=[WB["wfg_bf"][1]], writes=[B_Wfg])
    for kc_ in range(8):
        T.dma("sp", M("dma_start", out=Wfu[:, kc_, :], in_=WB["wfu_bf"][0][kc_ * 128:(kc_ + 1) * 128, :]), B_Wfu,
              reads=[WB["wfu_bf"][1]], writes=[B_Wfu])
    for kc_ in range(22):
        T.dma("sp", M("dma_start", out=Wfd[:, kc_, :], in_=WB["wfd_bf"][0][kc_ * 128:(kc_ + 1) * 128, :]), B_Wfd,
              reads=[WB["wfd_bf"][1]], writes=[B_Wfd])

    for v in range(8):
        for j in range(2):
            hkt, B_hk = hk[j]
            r0 = 256 * v + 128 * j

            def src(xtile, B_x, r0=r0, hkt=hkt, B_hk=B_hk, v=v, j=j):
                T.dma("sp", M("dma_start", out=xtile[:, :], in_=hbuf_d[r0:r0 + 128, :]), B_x,
                      reads=[B_hbuf[2 * v + j]], writes=[B_x])
                T.dma("sp", M("dma_start", out=hkt[:, :], in_=hbuf_d[r0:r0 + 128, :]), B_hk,
                      reads=[B_hbuf[2 * v + j]], writes=[B_hk])
            norm_tile(src, 128, hnT, B_hnT, 128 * j)
        for fb in range(NFB):
            gp, B_gp = PB[fb % 2]
            up, B_up = PB[2 + fb % 2]
            for dc in range(8):
                T.op("pe", M("matmul", out=gp[:, 0:256], lhsT=Wfg[:, dc, fb * 128:(fb + 1) * 128], rhs=hnT[:, dc, :],
                                                            start=(dc == 0), stop=(dc == 7)), reads=[B_Wfg, B_hnT], writes=[B_gp])
            for dc in range(8):
                T.op("pe", M("matmul", out=up[:, 0:256], lhsT=Wfu[:, dc, fb * 128:(fb + 1) * 128], rhs=hnT[:, dc, :],
                                                            start=(dc == 0), stop=(dc == 7)), reads=[B_Wfu, B_hnT], writes=[B_up])
            T.op("act", M("activation", out=sg[:, :], in_=gp[:, 0:256], func=AF.Silu), reads=[B_gp], writes=[B_sg])
            T.op("dve", M("tensor_tensor", out=hidT[:, fb, :], in0=sg[:, :], in1=up[:, 0:256], op=ALU.mult),
                 reads=[B_sg, B_up], writes=[B_hidT])
        for j in range(2):
            hkt, B_hk = hk[j]
            yt, B_yt = yo[j]
            for half in range(2):
                dn, B_dn = PB[4 + half]
                for fb in range(NFB):
                    T.op("pe", M("matmul", out=dn[:, :], lhsT=hidT[:, fb, 128 * j:128 * j + 128],
                                                                rhs=Wfd[:, fb, half * 512:(half + 1) * 512],
                                                                start=(fb == 0), stop=(fb == NFB - 1)),
                         reads=[B_hidT, B_Wfd], writes=[B_dn])
                T.op("dve", M("tensor_tensor", out=hkt[:, half * 512:(half + 1) * 512], in0=dn[:, :],
                                                             in1=hkt[:, half * 512:(half + 1) * 512], op=ALU.add),
                     reads=[B_dn, B_hk], writes=[B_hk])
            T.op("act", M("activation", out=sqj[:, :], in_=hkt[:, :], func=AF.Square, accum_out=stat[:, 4:5]),
                 reads=[B_hk], writes=[B_sqj, B_stat])
            T.op("act", M("activation", out=stat[:, 5:6], in_=stat[:, 4:5], func=AF.Ln, scale=1.0 / D, bias=cf[:, 280:281]),
                 reads=[B_stat, B_cf], writes=[B_stat])
            T.op("act", M("activation", out=stat[:, 6:7], in_=stat[:, 5:6], func=AF.Exp, scale=-0.5), reads=[B_stat], writes=[B_stat])
            T.op("dve", M("scalar_tensor_tensor", out=yt[:, :], in0=hkt[:, :], scalar=stat[:, 6:7], in1=gfin[:, :],
                                                         op0=ALU.mult, op1=ALU.mult), reads=[B_hk, B_stat, B_gfin], writes=[B_yt])
            r0 = 256 * v + 128 * j
            T.dma("sp", M("dma_start", out=y_d[r0:r0 + 128, :], in_=yt[:, :]), B_yt, reads=[B_yt], writes=[B_y])
    T.barrier()

    finish()
    p3.close()
    es.close()
    return nc


_NC = {}


def kernel(x, meta, norm_mix_g, w_in, w_gate, b_gate, lam_q1, lam_k1, lam_q2, lam_k2, subln_g,
           w_br_sb, w_br_df, w_out, norm_ffn_g, w_ffn_gate, w_ffn_up, w_ffn_down, norm_final_g):
    in_maps = prep(x, meta, norm_mix_g, w_in, w_gate, b_gate, lam_q1, lam_k1, lam_q2, lam_k2, subln_g,
                   w_br_sb, w_br_df, w_out, norm_ffn_g, w_ffn_gate, w_ffn_up, w_ffn_down, norm_final_g)
    if "nc" not in _NC:
        _NC["nc"] = build()
    nc = _NC["nc"]
    res = run_bass_kernel_spmd(nc, in_maps, core_ids=list(range(8)))
    out = np.empty((2, S, D), np.float32)
    for c in range(8):
        b, g = c // 4, c % 4
        out[b, 2048 * g:2048 * (g + 1)] = res.results[c]["y"]
    return out


def prep(x, meta, norm_mix_g, w_in, w_gate, b_gate, lam_q1, lam_k1, lam_q2, lam_k2, subln_g,
         w_br_sb, w_br_df, w_out, norm_ffn_g, w_ffn_gate, w_ffn_up, w_ffn_down, norm_final_g):
    f = lambda a: np.ascontiguousarray(np.asarray(a, dtype=np.float32))
    x = f(x)
    w_in = f(w_in)[0]
    lam = np.concatenate([f(lam_q1)[0], f(lam_k1)[0], f(lam_q2)[0], f(lam_k2)[0]])
    lam_b = np.ascontiguousarray(np.broadcast_to(lam[None, :], (128, 256)))
    gfin_b = np.ascontiguousarray(np.broadcast_to(f(norm_final_g)[None, :], (128, D)))
    common = {
        "meta": f(meta), "gmix": _pk(f(norm_mix_g)[0], 8), "wgate": f(w_gate)[0], "bgate": _pk(f(b_gate)[0], 16),
        "wsb": f(w_br_sb)[0], "wdf": f(w_br_df)[0], "wout": f(w_out)[0], "gffn": _pk(f(norm_ffn_g)[0], 8),
        "wfg": f(w_ffn_gate)[0], "wfu": f(w_ffn_up)[0], "wfd": f(w_ffn_down)[0], "gfin": gfin_b, "lam": lam_b,
        "subg": f(subln_g)[0].reshape(128, 1).copy(),
    }
    in_maps = []
    for c in range(8):
        b, g = c // 4, c % 4
        cols = []
        for h in (2 * g, 2 * g + 1):
            cols += [np.arange(h * 128, (h + 1) * 128), np.arange(1024 + h * 128, 1024 + (h + 1) * 128)]
        for h in (2 * g, 2 * g + 1):
            cols += [np.arange(3072 + h * 128, 3072 + (h + 1) * 128), np.arange(4096 + h * 128, 4096 + (h + 1) * 128)]
        for h in (2 * g, 2 * g + 1):
            cols += [np.arange(2048 + h * 128, 2048 + (h + 1) * 128)]
        for h in (2 * g, 2 * g + 1):
            cols += [np.arange(5120 + h * 128, 5120 + (h + 1) * 128)]
        w1 = np.ascontiguousarray(w_in[:, np.concatenate(cols)])
        cbv, c3v, cfv, kbv = _consts(g)
        m = dict(common)
        m.update({"xb": x[b], "x2": np.ascontiguousarray(x[b, 2048 * g:2048 * (g + 1)]), "w1": w1,
                  "cb": cbv, "c3": c3v, "cf": cfv, "kb": kbv})
        in_maps.append(m)
    return in_maps
```

```python
import contextlib
import math

import ml_dtypes
import numpy as np

import concourse.bass as bass
import concourse.mybir as mybir
from concourse.bass_utils import run_bass_kernel_spmd

F32 = mybir.dt.float32
BF = mybir.dt.bfloat16
AF = mybir.ActivationFunctionType
ALU = mybir.AluOpType
AX = mybir.AxisListType

D = 1024
S = 8192
NM = 16
L = S + NM
NST = 16
DFF = 2816
NFB = DFF // 128
EPS = 1e-6
SB_SCALE = 128 ** -0.5
DF_SCALE = 64 ** -0.5
LAM_INIT = 0.8 - 0.6 * math.exp(0.0)
BIGNEG = -30000.0
NCH = 8
SAME_ENG_SYNC = ("act", "dve", "pool")


def M(name, *a, **kw):
    return lambda e: getattr(e, name)(*a, **kw)


class Buf:
    def __init__(self, name):
        self.name = name
        self.w = None
        self.r = []
        self.dsem = None
        self.dcnt = 0


class Eng:
    def __init__(self, name):
        self.name = name
        self.sem = "e_" + name
        self.cnt = 0
        self.q = []


class Tracker:
    def __init__(self):
        self.eng = {n: Eng(n) for n in ("pe", "act", "dve", "pool", "sp")}
        self.semnames = [e.sem for e in self.eng.values()]
        self.bufs = []

    def buf(self, name):
        b = Buf(name)
        self.bufs.append(b)
        return b

    def _collect(self, eng, reads, writes):
        waits = {}

        def add(tok):
            if tok is None:
                return
            sem, val = tok
            if sem == eng.sem and eng.name not in SAME_ENG_SYNC:
                return
            if waits.get(sem, 0) < val:
                waits[sem] = val

        for b in reads:
            add(b.w)
        for b in writes:
            add(b.w)
            for t in b.r:
                add(t)
        return waits

    def _update(self, tok, reads, writes):
        for b in reads:
            b.r.append(tok)
        for b in writes:
            b.w = tok
            b.r = []

    def op(self, en, fn, reads=(), writes=()):
        eng = self.eng[en]
        waits = self._collect(eng, reads, writes)
        eng.cnt += 1
        tok = (eng.sem, eng.cnt)
        eng.q.append((waits, fn, (eng.sem, 1)))
        self._update(tok, reads, writes)
        return tok

    def dma(self, en, fn, sb, reads=(), writes=()):
        eng = self.eng[en]
        if sb.dsem is None:
            sb.dsem = "d_%d_%s" % (len(self.semnames), sb.name)
            self.semnames.append(sb.dsem)
        waits = self._collect(eng, reads, writes)
        sb.dcnt += 16
        tok = (sb.dsem, sb.dcnt)
        eng.q.append((waits, fn, (sb.dsem, 16)))
        self._update(tok, reads, writes)
        return tok

    def cc(self, fn, sem, reads=(), writes=()):
        eng = self.eng["pool"]
        waits = self._collect(eng, reads, writes)
        if sem not in self.semnames:
            self.semnames.append(sem)
        tok = (sem, 1)
        eng.q.append((waits, fn, (sem, 1)))
        self._update(tok, reads, writes)
        return tok

    def barrier(self, skip=()):
        toks = {}
        for e in self.eng.values():
            if e.cnt:
                toks[e.sem] = e.cnt
        for b in self.bufs:
            for t in [b.w] + b.r:
                if t is not None and toks.get(t[0], 0) < t[1]:
                    toks[t[0]] = t[1]
        for e in self.eng.values():
            w = {s: v for s, v in toks.items() if s != e.sem and not s.startswith(tuple(skip) or ("\0",))}
            e.q.append((w, None, None))

    def replay(self, en, e, sems):
        eng = self.eng[en]
        known = {}
        for waits, fn, inc in eng.q:
            for s, v in waits.items():
                if known.get(s, 0) < v:
                    e.wait_ge(sems[s], v)
                    known[s] = v
            if fn is None:
                continue
            ins = fn(e)
            ins.then_inc(sems[inc[0]], inc[1])


def _consts(g):
    bf = ml_dtypes.bfloat16
    i = np.arange(128)
    k = i[:, None]
    q = i[None, :]
    ident = (k == q).astype(np.float32)
    tri_i = (k >= q).astype(np.float32)
    tri_c = (k < q).astype(np.float32)
    negm = np.where(k >= q, BIGNEG, 0.0)
    bigp = np.where(k > q, 100.0, 0.0)
    sh = (k == q - 1).astype(np.float32) - (k == q).astype(np.float32)
    sel = ((k == 127) & (q == 0)).astype(np.float32)
    selm = ((k == 15) & (q == 0)).astype(np.float32)
    cb = np.zeros((128, 2048 + 512), np.float32)
    cb[:, 0:128] = ident
    cb[:, 128:256] = tri_i
    cb[:, 256:384] = tri_c
    cb[:, 384:512] = negm
    cb[:, 512:640] = bigp
    cb[:, 640:768] = -bigp
    cb[:, 768:896] = sh
    cb[:, 896:1024] = sel
    cb[:, 1024:1152] = selm
    c3 = np.zeros((3, 2 * 656), np.float32)
    cf = np.zeros((128, 128 + 2 * 80), np.float32)
    cf[:, 0:128] = 1.0
    for hh in range(2):
        h = 2 * g + hh
        slope = 2.0 ** (-(h + 1))
        c = slope / DF_SCALE
        bc = np.where(k > q, -2.0 * c * (k - q), 0.0)
        bc = np.where((k >= 64) & (q < 64), BIGNEG, bc)
        cb[:, 1152 + 128 * hh:1280 + 128 * hh] = bc
        o = 656 * hh
        c3[0, o:o + 128] = c * i
        c3[1, o:o + 128] = 1.0
        c3[2, o:o + 128] = 1.0
        c3[0, o + 128:o + 144] = c * (np.arange(16) - 16)
        c3[1, o + 128:o + 144] = 1.0
        c3[2, o + 128:o + 144] = 1.0
        j = np.arange(512)
        c3[0, o + 144:o + 656] = 1.0
        c3[1, o + 144:o + 656] = -c * (j % 128)
        c3[2, o + 144:o + 656] = -c * 128 * (j // 128)
        for dl in range(-3, 70):
            cf[:, 128 + 80 * hh + dl + 3] = -slope * 128.0 * dl
    kb = np.zeros((2, 3, L), np.float32)
    pos = np.arange(L)
    ii = np.where(pos < NM, pos - NM, (pos - NM) % 128)
    for hh in range(2):
        h = 2 * g + hh
        c = (2.0 ** (-(h + 1))) / DF_SCALE
        kb[hh, 0] = c * ii
        kb[hh, 1] = 1.0
        kb[hh, 2] = 1.0
    return cb.astype(bf), c3.astype(bf), cf, kb.astype(bf)


def _pk(v, n):
    return np.ascontiguousarray(np.asarray(v, np.float32).reshape(n, 128).T)


def build(stop=None):
    nc = bass.Bass("TRN2", target_bir_lowering=False)
    T = Tracker()

    def din(name, shape, dt=F32):
        return nc.dram_tensor(name, list(shape), dt, kind="ExternalInput").ap()

    xb_d = din("xb", [S, D])
    meta_d = din("meta", [NM, D])
    x2_d = din("x2", [2048, D])
    w1_d = din("w1", [D, 1536])
    gmix_d = din("gmix", [128, 8])
    wgate_d = din("wgate", [D, 2048])
    bgate_d = din("bgate", [128, 16])
    wsb_d = din("wsb", [D, D])
    wdf_d = din("wdf", [D, D])
    wout_d = din("wout", [D, D])
    gffn_d = din("gffn", [128, 8])
    wfg_d = din("wfg", [D, DFF])
    wfu_d = din("wfu", [D, DFF])
    wfd_d = din("wfd", [DFF, D])
    gfin_d = din("gfin", [128, D])
    lam_d = din("lam", [128, 256])
    subg_d = din("subg", [128, 1])
    cb_d = din("cb", [128, 2560], BF)
    c3_d = din("c3", [3, 1312], BF)
    cf_d = din("cf", [128, 288])
    kb_d = din("kb", [2, 3, L], BF)
    y_d = nc.dram_tensor("y", [2048, D], F32, kind="ExternalOutput").ap()
    arb_d = nc.dram_tensor("arb", [NCH * 4096, 1024], BF).ap()
    arg_d = nc.dram_tensor("arg", [NCH * 4096, 1024], BF).ap()
    hbuf_d = nc.dram_tensor("hbuf", [2048, D], F32).ap()
    own_d = nc.dram_tensor("own", [NCH * 512, 1024], BF).ap()
    osel_d = nc.dram_tensor("osel", [2 * 2048, 1024], BF).ap()

    B_arb = [T.buf("arb%d" % c) for c in range(NCH)]
    B_arg = [T.buf("arg%d" % c) for c in range(NCH)]
    B_hbuf = [T.buf("hbuf%d" % c) for c in range(16)]
    B_y = T.buf("y")
    B_own = [T.buf("own%d" % c) for c in range(NCH)]
    B_ownx = [T.buf("ownx%d" % c) for c in range(NCH)]
    B_osel = [T.buf("osel%d" % c) for c in range(2)]

    es = contextlib.ExitStack()

    def sb(name, shape, dt, stack=None):
        t = (stack or es).enter_context(nc.sbuf_tensor("s_" + name, list(shape), dt))
        return t, T.buf(name)

    def ps(name, shape, dt, stack=None):
        t = (stack or es).enter_context(nc.psum_tensor("p_" + name, list(shape), dt))
        return t, T.buf(name)

    pidc = {}

    def PID(e):
        if "p" not in pidc:
            pidc["p"] = e.partition_id()
        return pidc["p"]


    def finish():
        T.barrier()
        with contextlib.ExitStack() as ss:
            sems = {n: ss.enter_context(nc.semaphore(n)) for n in T.semnames}
            block = ss.enter_context(nc.Block())

            @block.tensor
            def _(e):
                T.replay("pe", e, sems)

            @block.scalar
            def _(e):
                T.replay("act", e, sems)

            @block.vector
            def _(e):
                T.replay("dve", e, sems)

            @block.gpsimd
            def _(e):
                T.replay("pool", e, sems)

            @block.sync
            def _(e):
                T.replay("sp", e, sems)

    def dump(name, src, B_src, rows, cols, dt):
        d = nc.dram_tensor(name, [rows, cols], dt, kind="ExternalOutput").ap()
        T.dma("sp", M("dma_start", out=d[:, :], in_=src), T.buf(name + "_s"), reads=[B_src], writes=[T.buf(name)])

    cb, B_cb = sb("cb", [128, 2560], BF)
    c3, B_c3 = sb("c3", [3, 1312], BF)
    cf, B_cf = sb("cf", [128, 288], F32)
    lamt, B_lam = sb("lamt", [128, 256], F32)
    lamw, B_lamw = sb("lamw", [128, 8], F32)
    subg, B_subg = sb("subg", [128, 1], F32)
    gmix, B_gmix = sb("gmix", [128, 8], F32)
    gffn, B_gffn = sb("gffn", [128, 8], F32)
    bgate, B_bgate = sb("bgate", [128, 16], F32)
    stat, B_stat = sb("stat", [128, 8], F32)
    xt = [sb("xt%d" % i, [128, D], F32) for i in range(2)]
    xnb, B_xnb = sb("xnb", [128, D], BF)
    sqj, B_sqj = sb("sqj", [128, D], BF)
    wstg = [sb("wstg%d" % i, [128, 1024], F32) for i in range(2)]
    PB = [ps("pb%d" % i, [128, 512], F32) for i in range(7)]
    tp, B_tp = ps("tp", [128, 8, 128], BF)

    ident = cb[:, 0:128]
    triI = cb[:, 128:256]
    triC = cb[:, 256:384]
    negm = cb[:, 384:512]
    bigp = cb[:, 512:640]
    bigpn = cb[:, 640:768]
    shm = cb[:, 768:896]
    selc = cb[:, 896:1024]
    selmc = cb[:, 1024:1152]
    zer = cb[:, 2048:2560]
    onesf = cf[:, 0:128]

    T.dma("sp", M("dma_start", out=cb[:, :], in_=cb_d[:, :]), B_cb, writes=[B_cb])
    T.dma("sp", M("dma_start", out=c3[:, :], in_=c3_d[:, :]), B_c3, writes=[B_c3])
    T.dma("sp", M("dma_start", out=cf[:, :], in_=cf_d[:, :]), B_cf, writes=[B_cf])
    T.dma("sp", M("dma_start", out=lamt[:, :], in_=lam_d[:, :]), B_lam, writes=[B_lam])
    T.dma("sp", M("dma_start", out=subg[:, :], in_=subg_d[:, :]), B_subg, writes=[B_subg])
    T.dma("sp", M("dma_start", out=gmix[:, :], in_=gmix_d[:, :]), B_gmix, writes=[B_gmix])
    T.dma("sp", M("dma_start", out=gffn[:, :], in_=gffn_d[:, :]), B_gffn, writes=[B_gffn])
    T.dma("sp", M("dma_start", out=bgate[:, :], in_=bgate_d[:, :]), B_bgate, writes=[B_bgate])

    T.op("dve", M("tensor_tensor", out=lamt[:, 0:64], in0=lamt[:, 0:64], in1=lamt[:, 64:128], op=ALU.mult),
         reads=[B_lam], writes=[B_lam])
    T.op("dve", M("tensor_tensor", out=lamt[:, 128:192], in0=lamt[:, 128:192], in1=lamt[:, 192:256], op=ALU.mult),
         reads=[B_lam], writes=[B_lam])
    T.op("dve", M("reduce_sum", out=lamw[:, 0:1], in_=lamt[:, 0:64], axis=AX.X), reads=[B_lam], writes=[B_lamw])
    T.op("dve", M("reduce_sum", out=lamw[:, 1:2], in_=lamt[:, 128:192], axis=AX.X), reads=[B_lam], writes=[B_lamw])
    T.op("act", M("activation", out=lamw[:, 2:4], in_=lamw[:, 0:2], func=AF.Exp), reads=[B_lamw], writes=[B_lamw])
    T.op("dve", M("tensor_tensor", out=lamw[:, 4:5], in0=lamw[:, 3:4], in1=lamw[:, 2:3], op=ALU.subtract),
         reads=[B_lamw], writes=[B_lamw])
    T.op("dve", M("tensor_scalar", out=lamw[:, 4:5], in0=lamw[:, 4:5], scalar1=-LAM_INIT, scalar2=None, op0=ALU.add),
         reads=[B_lamw], writes=[B_lamw])
    T.op("dve", M("tensor_scalar", out=subg[:, 0:1], in0=subg[:, 0:1], scalar1=1.0 - LAM_INIT, scalar2=None, op0=ALU.mult),
         reads=[B_subg], writes=[B_subg])

    wq = {"i": 0}

    def load_w(src, k0, nk, c0, ncol, dst, B_dst, dk0, dc0, gain):
        for kc in range(nk):
            for cc in range(0, ncol, 1024):
                w = min(1024, ncol - cc)
                st, B_st = wstg[wq["i"] % 2]
                wq["i"] += 1
                r0 = (k0 + kc) * 128
                a = c0 + cc
                T.dma("sp", M("dma_start", out=st[:, 0:w], in_=src[r0:r0 + 128, a:a + w]),
                      B_st, writes=[B_st])
                if gain is None:
                    T.op("dve", M("tensor_copy",
                        out=dst[:, dk0 + kc, dc0 + cc:dc0 + cc + w], in_=st[:, 0:w]), reads=[B_st], writes=[B_dst])
                else:
                    gt, B_g = gain
                    T.op("dve", M("tensor_scalar",
                        out=dst[:, dk0 + kc, dc0 + cc:dc0 + cc + w], in0=st[:, 0:w],
                        scalar1=gt[:, k0 + kc:k0 + kc + 1], scalar2=None, op0=ALU.mult),
                        reads=[B_st, B_g], writes=[B_dst])

    xq = {"i": 0}

    def norm_tile(src_fn, rows, dst, B_dst, col0, src_reads=()):
        xtile, B_x = xt[xq["i"] % 2]
        xq["i"] += 1
        src_fn(xtile, B_x)
        T.op("act", M("activation", out=sqj[0:rows, :], in_=xtile[0:rows, :], func=AF.Square,
                                           accum_out=stat[0:rows, 0:1]), reads=[B_x], writes=[B_sqj, B_stat])
        T.op("act", M("activation", out=stat[0:rows, 1:2], in_=stat[0:rows, 0:1], func=AF.Ln,
                                           scale=1.0 / D, bias=cf[0:rows, 280:281]), reads=[B_stat, B_cf], writes=[B_stat])
        T.op("act", M("activation", out=stat[0:rows, 2:3], in_=stat[0:rows, 1:2], func=AF.Exp, scale=-0.5),
             reads=[B_stat], writes=[B_stat])
        T.op("dve", M("tensor_scalar", out=xnb[0:rows, :], in0=xtile[0:rows, :], scalar1=stat[0:rows, 2:3],
                                              scalar2=None, op0=ALU.mult), reads=[B_x, B_stat], writes=[B_xnb])
        for c in range(8):
            T.op("pe", M("transpose", out=tp[:, c, 0:rows], in_=xnb[0:rows, c * 128:(c + 1) * 128],
                                                 identity=ident[0:rows, 0:rows]), reads=[B_xnb, B_cb], writes=[B_tp])
        T.op("dve", M("tensor_copy", out=dst[:, :, col0:col0 + rows], in_=tp[:, :, 0:rows]),
             reads=[B_tp], writes=[B_dst])
        return xtile, B_x

    p1 = contextlib.ExitStack()
    GRPS = list(range(-1, NST))
    wb, B_wb = sb("wb", [128, 8, 768], BF, p1)
    KT4 = [sb("KT%d" % a, [128, L], BF, p1)[0] for a in range(4)]
    KTB = [{g_: T.buf("KT%d_%d" % (a, g_)) for g_ in GRPS} for a in range(4)]
    TMt = [sb("TM%d" % h, [128, 65, 128], BF, p1)[0] for h in range(2)]
    TMB = [{g_: T.buf("TM%d_%d" % (h, g_)) for g_ in GRPS} for h in range(2)]
    QT = [[sb("QT%d_%d" % (h, par), [128, 512], BF, p1) for par in range(2)] for h in range(2)]
    VT = [[sb("VT%d_%d" % (h, par), [128, 512], BF, p1) for par in range(2)] for h in range(2)]
    QTd = [[[sb("QTd%d_%d_%d" % (h, c, par), [128, 512], BF, p1) for par in range(2)] for c in range(2)] for h in range(2)]
    xnT, B_xnT = sb("xnT", [128, 8, 512], BF, p1)
    vtok = [sb("vtok%d" % i, [128, 256], BF, p1) for i in range(2)]
    Ub = [sb("U%d" % h, [128, 512], BF, p1) for h in range(2)]
    Pb = [sb("P%d" % h, [128, 512], BF, p1) for h in range(2)]
    Eb = [[sb("E%d_%d" % (par, c), [128, 512], BF, p1) for c in range(2)] for par in range(2)]
    Eacc1 = [[sb("Eacc%d_%d" % (h, c), [128, 512], F32, p1) for c in range(2)] for h in range(2)]
    Eacc = [Eacc1, Eacc1]
    odsb = [[sb("od%d_%d" % (h, c), [128, 512], F32, p1) for c in range(2)] for h in range(2)]
    osb = [sb("osb%d" % h, [128, 512], BF, p1) for h in range(2)]
    t0, B_t0 = sb("t0", [128, 512], F32, p1)
    t1, B_t1 = sb("t1", [128, 512], F32, p1)
    t2, B_t2 = sb("t2", [128, 512], F32, p1)
    zt, B_zt = sb("zt", [128, 1024], BF, p1)
    wcv = [sb("wcv%d" % i, [128, 1024], BF, p1) for i in range(2)]

    def grp_of_kt(kt):
        return -1 if kt is None else kt // 4

    T.op("dve", M("memset", cf[:, 280:281], EPS), reads=[B_cf], writes=[B_cf])
    T.op("dve", M("memset", zt[:, :], 0.0), writes=[B_zt])
    if stop is not None:
        for h in range(2):
            T.op("dve", M("memset", TMt[h][:, 0, :], 0.0), writes=[TMB[h][-1]])

    def zfill_gen():
        tok = None
        for c in range(NCH):
            for blk in range(32):
                r0 = c * 4096 + blk * 128
                tok = T.dma("pool", M("dma_start", out=arb_d[r0:r0 + 128, :], in_=zt[:, :]), B_zt, reads=[B_zt])
                if blk % 2 == 1:
                    yield
        for c in range(NCH):
            B_arb[c].w = tok

    bg = []

    def pump(k=1):
        for _ in range(k):
            for g_ in list(bg):
                try:
                    next(g_)
                except StopIteration:
                    bg.remove(g_)

    def drain(g_):
        for _ in g_:
            pass
        if g_ in bg:
            bg.remove(g_)

    def norm_gen(src_fn, rows, dst, B_dst, col0):
        xtile, B_x = xt[xq["i"] % 2]
        xq["i"] += 1
        src_fn(xtile, B_x)
        yield
        T.op("act", M("activation", out=sqj[0:rows, :], in_=xtile[0:rows, :], func=AF.Square,
                      accum_out=stat[0:rows, 0:1]), reads=[B_x], writes=[B_sqj, B_stat])
        T.op("act", M("activation", out=stat[0:rows, 1:2], in_=stat[0:rows, 0:1], func=AF.Ln,
                      scale=1.0 / D, bias=cf[0:rows, 280:281]), reads=[B_stat, B_cf], writes=[B_stat])
        T.op("act", M("activation", out=stat[0:rows, 2:3], in_=stat[0:rows, 1:2], func=AF.Exp, scale=-0.5),
             reads=[B_stat], writes=[B_stat])
        yield
        T.op("dve", M("tensor_scalar", out=xnb[0:rows, :], in0=xtile[0:rows, :], scalar1=stat[0:rows, 2:3],
                      scalar2=None, op0=ALU.mult), reads=[B_x, B_stat], writes=[B_xnb])
        yield
        for c in range(8):
            T.op("pe", M("transpose", out=tp[:, c, 0:rows], in_=xnb[0:rows, c * 128:(c + 1) * 128],
                         identity=ident[0:rows, 0:rows]), reads=[B_xnb, B_cb], writes=[B_tp])
        yield
        T.op("dve", M("tensor_copy", out=dst[:, :, col0:col0 + rows], in_=tp[:, :, 0:rows]),
             reads=[B_tp], writes=[B_dst])
        yield

    def front_gen(pass_id, grp):
        if grp < 0:
            tiles = [(16, 0)]
            n = 16
            pos0 = 0
        else:
            tiles = [(128, j * 128) for j in range(4)]
            n = 512
            pos0 = NM + 512 * grp
        par = grp % 2
        for j, (rows, col0) in enumerate(tiles):
            if grp < 0:
                src = lambda xtile, B_x: T.dma("sp", M("dma_start", out=xtile[0:16, :], in_=meta_d[:, :]), B_x, writes=[B_x])
            else:
                r0 = 512 * grp + 128 * j
                src = lambda xtile, B_x, r0=r0: T.dma("sp", M("dma_start", out=xtile[:, :], in_=xb_d[r0:r0 + 128, :]), B_x, writes=[B_x])
            yield from norm_gen(src, rows, xnT, B_xnT, col0)
        pj, B_pj = PB[6]
        for h in range(2):
            for kind in range(2 if pass_id == 1 else 0):
                for c in range(2):
                    c0 = (2 * h + kind) * 128 + 64 * c
                    if kind == 0 and grp < 0:
                        continue
                    for dc in range(8):
                        T.op("pe", M("matmul", out=pj[0:64, 0:n], lhsT=wb[:, dc, c0:c0 + 64], rhs=xnT[:, dc, 0:n],
                                     start=(dc == 0), stop=(dc == 7)), reads=[B_wb, B_xnT], writes=[B_pj])
                    yield
                    if kind == 0:
                        qd, B_qd = QTd[h][c][par]
                        T.op("act", M("copy", out=qd[0:64, 0:n], in_=pj[0:64, 0:n]), reads=[B_pj], writes=[B_qd])
                    else:
                        T.op("act", M("copy", out=KT4[2 * h + c][0:64, pos0:pos0 + n], in_=pj[0:64, 0:n]),
                             reads=[B_pj], writes=[KTB[2 * h + c][grp]])
                    yield
            for kind in range(2 if pass_id == 0 else 0):
                c0 = (2 * h + kind) * 128
                if kind == 0 and grp < 0:
                    continue
                for dc in range(8):
                    T.op("pe", M("matmul", out=pj[:, 0:n], lhsT=wb[:, dc, c0:c0 + 128], rhs=xnT[:, dc, 0:n],
                                 start=(dc == 0), stop=(dc == 7)), reads=[B_wb, B_xnT], writes=[B_pj])
                yield
                if kind == 0:
                    qq, B_qq = QT[h][par]
                    T.op("dve", M("tensor_copy", out=qq[:, 0:n], in_=pj[:, 0:n]), reads=[B_pj], writes=[B_qq])
                else:
                    T.op("dve", M("tensor_copy", out=KT4[h][:, pos0:pos0 + n], in_=pj[:, 0:n]), reads=[B_pj], writes=[KTB[h][grp]])
                yield
            if pass_id == 0 and grp >= 0:
                c0 = 512 + h * 128
                for dc in range(8):
                    T.op("pe", M("matmul", out=pj[:, 0:n], lhsT=wb[:, dc, c0:c0 + 128], rhs=xnT[:, dc, 0:n],
                                 start=(dc == 0), stop=(dc == 7)), reads=[B_wb, B_xnT], writes=[B_pj])
                yield
                vv, B_vv = VT[h][par]
                T.op("dve", M("tensor_copy", out=vv[:, 0:n], in_=pj[:, 0:n]), reads=[B_pj], writes=[B_vv])
                yield
        for j, (rows, col0) in enumerate(tiles):
            ti = 0 if grp < 0 else 1 + 4 * grp + j
            for dc in range(8):
                T.op("pe", M("matmul", out=pj[0:rows, 0:256], lhsT=xnT[:, dc, col0:col0 + rows], rhs=wb[:, dc, 512:768],
                             start=(dc == 0), stop=(dc == 7)), reads=[B_wb, B_xnT], writes=[B_pj])
            yield
            if pass_id == 1:
                for h in range(2):
                    T.op("act", M("copy", out=TMt[h][0:rows, ti, :], in_=pj[0:rows, h * 128:(h + 1) * 128]),
                         reads=[B_pj], writes=[TMB[h][grp]])
                yield
            else:
                vc, B_vc = vtok[ti % 2]
                vp, B_vp = vtok[(ti + 1) % 2]
                T.op("dve", M("tensor_copy", out=vc[0:rows, :], in_=pj[0:rows, 0:256]), reads=[B_pj], writes=[B_vc])
                yield
                T.op("pe", M("matmul", out=pj[0:rows, 256:512], lhsT=shm[0:rows, 0:rows], rhs=vc[0:rows, :],
                             start=True, stop=True), reads=[B_vc, B_cb], writes=[B_pj])
                if ti == 1:
                    T.op("pe", M("matmul", out=pj[:, 256:512], lhsT=selmc[0:16, :], rhs=vp[0:16, :], start=False, stop=True,
                                 skip_group_check=True), reads=[B_vp, B_cb], writes=[B_pj])
                elif ti > 1:
                    T.op("pe", M("matmul", out=pj[:, 256:512], lhsT=selc[:, :], rhs=vp[:, :], start=False, stop=True,
                                 skip_group_check=True), reads=[B_vp, B_cb], writes=[B_pj])
                yield
                for h in range(2):
                    T.op("dve", M("tensor_copy", out=TMt[h][0:rows, ti, :], in_=pj[0:rows, 256 + h * 128:256 + (h + 1) * 128]),
                         reads=[B_pj], writes=[TMB[h][grp]])
                yield

    def key_tiles(s):
        lst = []
        for kt in range(4 * s + 3, -1, -1):
            a = kt - 4 * s
            lst.append((128, NM + 128 * kt, 1 + kt, 128 * a if a >= 0 else 0, a >= 0, kt))
        lst.append((16, 0, 0, 0, False, None))
        return lst

    def store_o(s, row_in_slot, src, B_src):
        c = s // 2
        col = (s % 2) * 512
        r0 = c * 512 + row_in_slot
        T.dma("pool", M("dma_start", out=own_d[r0:r0 + 128, col:col + 512], in_=src[:, :]),
              B_src, reads=[B_src], writes=[B_own[c]])

    WB = {}

    def wconv_gen(name, src, K, N, gain):
        dst = nc.dram_tensor(name, [128, (K // 128) * N], BF).ap()
        B_d = T.buf(name)
        WB[name] = (dst, B_d)
        i = 0
        for kc in range(K // 128):
            for cc in range(0, N, 1024):
                w = min(1024, N - cc)
                st, B_st = wstg[wq["i"] % 2]
                cv, B_cv = wcv[wq["i"] % 2]
                wq["i"] += 1
                r0 = kc * 128
                T.dma("sp", M("dma_start", out=st[:, 0:w], in_=src[r0:r0 + 128, cc:cc + w]), B_st, writes=[B_st])
                yield
                if gain is None:
                    T.op("dve", M("tensor_copy", out=cv[:, 0:w], in_=st[:, 0:w]), reads=[B_st], writes=[B_cv])
                else:
                    gt, B_g = gain
                    T.op("dve", M("tensor_scalar", out=cv[:, 0:w], in0=st[:, 0:w], scalar1=gt[:, kc:kc + 1], scalar2=None,
                                  op0=ALU.mult), reads=[B_st, B_g], writes=[B_cv])
                yield
                T.dma("sp", M("dma_start", out=dst[:, kc * N + cc:kc * N + cc + w], in_=cv[:, 0:w]), B_cv, reads=[B_cv], writes=[B_d])
                yield

    def wconv_all():
        yield from wconv_gen("wsb_bf", wsb_d, D, D, None)
        yield from wconv_gen("wdf_bf", wdf_d, D, D, None)
        yield from wconv_gen("wg_bf", wgate_d, D, 2048, (gmix, B_gmix))
        yield from wconv_gen("wo_bf", wout_d, D, D, None)
        yield from wconv_gen("wfg_bf", wfg_d, D, DFF, (gffn, B_gffn))
        yield from wconv_gen("wfu_bf", wfu_d, D, DFF, (gffn, B_gffn))
        yield from wconv_gen("wfd_bf", wfd_d, DFF, D, None)

    load_w(w1_d, 0, 8, 0, 512, wb, B_wb, 0, 0, (gmix, B_gmix))
    load_w(w1_d, 0, 8, 1024, 256, wb, B_wb, 0, 512, (gmix, B_gmix))

    def sb_attention(s, rate):
        tiles = key_tiles(s)
        par = s % 2
        zS = [PB[0], PB[1]]
        Rb = [PB[2], PB[3]]
        oS = [PB[4], PB[5]]
        for h in range(2):
            T.op("pe", M("matmul", out=Rb[h][0][:, :], lhsT=zer[:, 0:128], rhs=zer[:, :], start=True, stop=True),
                 reads=[B_cb], writes=[Rb[h][1]])
            T.op("pe", M("matmul", out=oS[h][0][:, :], lhsT=zer[:, 0:128], rhs=zer[:, :], start=True, stop=True),
                 reads=[B_cb], writes=[oS[h][1]])

        def qk(n, h):
            ksz, kc0, ti, q0, diag, kt = tiles[n]
            z, B_z = zS[h]
            qq, B_qq = QT[h][par]
            T.op("pe", M("matmul", out=z[0:ksz, q0:512], lhsT=KT4[h][:, kc0:kc0 + ksz], rhs=qq[:, q0:512],
                         start=True, stop=True), reads=[KTB[h][grp_of_kt(kt)], B_qq], writes=[B_z])
            if diag:
                T.op("pe", M("matmul", out=z[:, q0:q0 + 128], lhsT=ident, rhs=negm, start=False, stop=True, skip_group_check=True),
                     reads=[B_cb], writes=[B_z])

        for h in range(2):
            qk(0, h)
        for n in range(len(tiles)):
            ksz, kc0, ti, q0, diag, kt = tiles[n]
            last = n == len(tiles) - 1
            for h in range(2):
                z, B_z = zS[h]
                T.op("act", M("activation", out=z[0:ksz, q0:512], in_=z[0:ksz, q0:512], func=AF.Exp, scale=SB_SCALE),
                     reads=[B_z], writes=[B_z])
            for h in range(2):
                z, B_z = zS[h]
                U, B_U = Ub[h]
                R, B_R = Rb[h]
                T.op("act", M("activation", out=U[0:ksz, q0:512], in_=z[0:ksz, q0:512], func=AF.Ln, bias=1.0),
                     reads=[B_z], writes=[B_U])
                T.op("pe", M("matmul", out=R[0:ksz, q0:512], lhsT=triI[0:ksz, 0:ksz], rhs=U[0:ksz, q0:512],
                             start=False, stop=True, skip_group_check=True), reads=[B_U, B_cb], writes=[B_R])
                if diag:
                    T.op("pe", M("matmul", out=R[:, q0:q0 + 128], lhsT=ident, rhs=bigp, start=False, stop=True, skip_group_check=True),
                         reads=[B_cb], writes=[B_R])
                if not last:
                    qk(n + 1, h)
            pump(rate)
            for h in range(2):
                U, B_U = Ub[h]
                R, B_R = Rb[h]
                P, B_P = Pb[h]
                o, B_o = oS[h]
                T.op("act", M("activation", out=P[0:ksz, q0:512], in_=R[0:ksz, q0:512], func=AF.Exp, scale=-1.0),
                     reads=[B_R], writes=[B_P])
                if not last:
                    T.op("pe", M("matmul", out=R[:, q0:512], lhsT=triC[:, :], rhs=U[:, q0:512], start=False, stop=True,
                                 skip_group_check=True), reads=[B_U, B_cb], writes=[B_R])
                    if diag:
                        T.op("pe", M("matmul", out=R[:, q0:q0 + 128], lhsT=ident, rhs=bigpn, start=False, stop=True,
                                     skip_group_check=True), reads=[B_cb], writes=[B_R])
                T.op("pe", M("matmul", out=o[:, q0:512], lhsT=TMt[h][0:ksz, ti, :], rhs=P[0:ksz, q0:512],
                             start=False, stop=True, skip_group_check=True), reads=[TMB[h][grp_of_kt(kt)], B_P], writes=[B_o])
        for h in range(2):
            o, B_o = oS[h]
            ob, B_ob = osb[h]
            vv, B_vv = VT[h][par]
            T.op("dve", M("tensor_tensor", out=ob[:, :], in0=o[:, :], in1=vv[:, :], op=ALU.add),
                 reads=[B_o, B_vv], writes=[B_ob])
            store_o(s, h * 128, ob, B_ob)

    NFRONT = 64
    inter = stop is None or stop in ("sb1", "df1")
    stop_s = int(stop[2:]) if stop is not None and stop[:2] in ("sb", "df") else None
    sA = stop is not None and stop[:2] == "df"

    if not sA:
        drain(front_gen(0, -1))
        drain(front_gen(0, 0))
        if inter:
            bg.append(wconv_all())
        if stop is None:
            bg.append(zfill_gen())
    for s in range(NST if not sA else 0):
        if stop == "front":
            dump("dbg_kt", KT4[0][:, 0:528], KTB[0][0], 128, 528, BF)
            dump("dbg_tm", TMt[0][:, 0:5, :], TMB[0][0], 128, 640, BF)
            dump("dbg_qt", QT[0][0][0][:, :], QT[0][0][1], 128, 512, BF)
            dump("dbg_vt", VT[0][0][0][:, :], VT[0][0][1], 128, 512, F32)
            finish()
            return nc
        fg = None
        if s + 1 < NST and inter:
            fg = front_gen(0, s + 1)
            bg.insert(0, fg)
        nsteps = 4 * s + 5
        sb_attention(s, -(-NFRONT // nsteps))
        if fg is not None:
            drain(fg)
        if stop is not None and stop[:2] == "sb" and s == stop_s:
            dump("dbg_o0", osb[0][0][:, :], osb[0][1], 128, 512, BF)
            dump("dbg_o1", osb[1][0][:, :], osb[1][1], 128, 512, BF)
            finish()
            return nc
    for g_ in list(bg):
        drain(g_)

    load_w(w1_d, 0, 8, 512, 512, wb, B_wb, 0, 0, (gmix, B_gmix))
    load_w(w1_d, 0, 8, 1280, 256, wb, B_wb, 0, 512, (gmix, B_gmix))

    def df_attention(s, rate, prev_eg=None):
        tiles = key_tiles(s)
        par = s % 2
        zD = [PB[0], PB[1]]
        oD = [[PB[2], PB[3]], [PB[4], PB[5]]]
        for h in range(2):
            for c in range(2):
                T.op("pe", M("matmul", out=oD[h][c][0][:, :], lhsT=zer[:, 0:128], rhs=zer[:, :], start=True, stop=True),
                     reads=[B_cb], writes=[oD[h][c][1]])
                T.op("dve", M("memset", Eacc[par][h][c][0][:, :], 0.0), writes=[Eacc[par][h][c][1]])
        steps = [(n, h) for n in range(len(tiles)) for h in range(2)]

        def qk(m):
            n, h = steps[m]
            ksz, kc0, ti, q0, diag, kt = tiles[n]
            for c in range(2):
                z, B_z = zD[c]
                qd, B_qd = QTd[h][c][par]
                T.op("pe", M("matmul", out=z[0:ksz, q0:512], lhsT=KT4[2 * h + c][:, kc0:kc0 + ksz],
                             rhs=qd[:, q0:512], start=True, stop=True),
                     reads=[KTB[2 * h + c][grp_of_kt(kt)], B_qd], writes=[B_z])
                if diag:
                    T.op("pe", M("matmul", out=z[:, q0:q0 + 128], lhsT=ident, rhs=cb[:, 1152 + 128 * h:1280 + 128 * h],
                                 start=False, stop=True, skip_group_check=True), reads=[B_cb], writes=[B_z])

        def av(m):
            n, h = steps[m]
            ksz, kc0, ti, q0, diag, kt = tiles[n]
            for c in range(2):
                E, B_E = Eb[m % 2][c]
                o, B_o = oD[h][c]
                T.op("pe", M("matmul", out=o[:, q0:512], lhsT=TMt[h][0:ksz, ti, :], rhs=E[0:ksz, q0:512],
                             start=False, stop=True, skip_group_check=True), reads=[TMB[h][grp_of_kt(kt)], B_E], writes=[B_o])

        qk(0)
        for m in range(len(steps)):
            n, h = steps[m]
            ksz, kc0, ti, q0, diag, kt = tiles[n]
            dl = (4 * s - kt) if kt is not None else 4 * s
            kcol = 128 + 80 * h + dl + 3
            for c in range(2):
                z, B_z = zD[c]
                E, B_E = Eb[m % 2][c]
                T.op("act", M("activation", out=E[0:ksz, q0:512], in_=z[0:ksz, q0:512], func=AF.Exp,
                              scale=DF_SCALE, bias=cf[0:ksz, kcol:kcol + 1]), reads=[B_z, B_cf], writes=[B_E])
                ea, B_ea = Eacc[par][h][c]
                T.op("dve", M("tensor_tensor", out=ea[0:ksz, q0:512], in0=ea[0:ksz, q0:512], in1=E[0:ksz, q0:512], op=ALU.add),
                     reads=[B_E, B_ea], writes=[B_ea])
            if m + 1 < len(steps):
                qk(m + 1)
            av(m)
            if h == 1:
                pump(rate)
        if prev_eg is not None:
            drain(prev_eg)
        sm, B_sm = PB[6]
        for h in range(2):
            for c in range(2):
                ea, B_ea = Eacc[par][h][c]
                od, B_od = odsb[h][c]
                T.op("pe", M("matmul", out=sm[:, :], lhsT=onesf, rhs=ea[:, :], start=True, stop=True),
                     reads=[B_ea, B_cf], writes=[B_sm])
                T.op("dve", M("reciprocal", out=od[:, :], in_=sm[:, :]), reads=[B_sm], writes=[B_od])
                T.op("dve", M("tensor_tensor", out=od[:, :], in0=oD[h][c][0][:, :], in1=od[:, :], op=ALU.mult),
                     reads=[oD[h][c][1], B_od], writes=[B_od])

    def df_epilogue(s):
        sm, B_sm = PB[6]
        for h in range(2):
            T.op("dve", M("scalar_tensor_tensor", out=t0[:, :], in0=odsb[h][1][0][:, :], scalar=lamw[:, 4:5], in1=odsb[h][0][0][:, :],
                          op0=ALU.mult, op1=ALU.add), reads=[odsb[h][0][1], odsb[h][1][1], B_lamw], writes=[B_t0])
            T.op("dve", M("tensor_tensor", out=t1[:, :], in0=t0[:, :], in1=t0[:, :], op=ALU.mult), reads=[B_t0], writes=[B_t1])
            yield
            T.op("pe", M("matmul", out=sm[:, :], lhsT=onesf, rhs=t1[:, :], start=True, stop=True),
                 reads=[B_t1, B_cf], writes=[B_sm])
            yield
            T.op("act", M("activation", out=t2[:, :], in_=sm[:, :], func=AF.Ln, scale=1.0 / 128.0, bias=cf[:, 280:281]),
                 reads=[B_sm, B_cf], writes=[B_t2])
            T.op("act", M("activation", out=t2[:, :], in_=t2[:, :], func=AF.Exp, scale=-0.5), reads=[B_t2], writes=[B_t2])
            yield
            T.op("dve", M("tensor_tensor", out=t2[:, :], in0=t2[:, :], in1=t0[:, :], op=ALU.mult), reads=[B_t0, B_t2], writes=[B_t2])
            ob, B_ob = osb[h]
            T.op("dve", M("tensor_scalar", out=ob[:, :], in0=t2[:, :], scalar1=subg[:, 0:1], scalar2=None, op0=ALU.mult),
                 reads=[B_t2, B_subg], writes=[B_ob])
            store_o(s, 256 + h * 128, ob, B_ob)
            yield
        if s % 2 == 1 and stop is None:
            c = s // 2
            T.dma("pool", lambda e, c=c: e.dma_start(
                out=arb_d[bass.ds(PID(e) * 512 + c * 4096, 512), :], in_=own_d[c * 512:(c + 1) * 512, :]),
                B_ownx[c], reads=[B_own[c]], writes=[B_arb[c]])
            T.cc(M("collective_compute", "AllReduce", ALU.add, replica_groups=[list(range(8))],
                   ins=[arb_d[c * 4096:(c + 1) * 4096, :]], outs=[arg_d[c * 4096:(c + 1) * 4096, :]]),
                 "cc%d" % c, reads=[B_arb[c]], writes=[B_arg[c]])

    allg = [g_ for g_ in GRPS]
    for h in range(2):
        for c in range(2):
            T.op("dve", M("memset", KT4[2 * h + c][64:128, :], 0.0), writes=[KTB[2 * h + c][g_] for g_ in allg])
            T.dma("sp", M("dma_start", out=KT4[2 * h + c][64:67, :], in_=kb_d[h, :, :]), KTB[2 * h + c][-1],
                  writes=[KTB[2 * h + c][g_] for g_ in allg])
            for par in range(2):
                qd, B_qd = QTd[h][c][par]
                T.op("dve", M("memset", qd[64:128, :], 0.0), writes=[B_qd])
                T.dma("sp", M("dma_start", out=qd[64:67, :], in_=c3_d[:, 656 * h + 144:656 * h + 656]), B_qd, writes=[B_qd])
    drain(front_gen(1, -1))
    drain(front_gen(1, 0))
    prev_eg = None
    for s in range(NST):
        fg = None
        if s + 1 < NST and inter:
            fg = front_gen(1, s + 1)
            bg.insert(0, fg)
        nsteps = 4 * s + 5
        df_attention(s, -(-NFRONT // nsteps), prev_eg)
        if fg is not None:
            drain(fg)
        eg = df_epilogue(s)
        prev_eg = eg
        if sA and s == stop_s:
            drain(eg)
            dump("dbg_o0", osb[0][0][:, :], osb[0][1], 128, 512, BF)
            dump("dbg_o1", osb[1][0][:, :], osb[1][1], 128, 512, BF)
            finish()
            return nc
        bg.append(eg)
    for g_ in list(bg):
        drain(g_)

    T.barrier(skip=("cc",))
    p1.close()

    p2 = contextlib.ExitStack()
    Wsb, B_Wsb = sb("Wsb", [128, 8, 1024], BF, p2)
    Wdf, B_Wdf = sb("Wdf", [128, 8, 1024], BF, p2)
    Wg, B_Wg = sb("Wg", [128, 8, 2048], BF, p2)
    Wo, B_Wo = sb("Wo", [128, 8, 1024], BF, p2)
    oT, B_oT = sb("oT", [128, 16, 512], BF, p2)
    xnT2, B_xnT2 = sb("xnT2", [128, 8, 512], BF, p2)
    mgT, B_mgT = sb("mgT", [128, 8, 512], BF, p2)
    xk = [sb("xk%d" % j, [128, D], F32, p2) for j in range(4)]
    g0, B_g0 = sb("g0", [128, 512], F32, p2)
    g1, B_g1 = sb("g1", [128, 512], F32, p2)
    ht = [sb("ht%d" % j, [128, D], F32, p2) for j in range(2)]
    for kh_ in range(2):
        k0_, k1_ = kh_ * (8 // 2), (kh_ + 1) * (8 // 2)
        nn_ = Wsb.shape[2]
        T.dma("sp", M("dma_start", out=Wsb[:, k0_:k1_, :],
                      in_=WB["wsb_bf"][0][:, k0_ * nn_:k1_ * nn_].rearrange("p (k n) -> p k n", n=nn_)), B_Wsb,
              reads=[WB["wsb_bf"][1]], writes=[B_Wsb])
    for kh_ in range(2):
        k0_, k1_ = kh_ * (8 // 2), (kh_ + 1) * (8 // 2)
        nn_ = Wdf.shape[2]
        T.dma("sp", M("dma_start", out=Wdf[:, k0_:k1_, :],
                      in_=WB["wdf_bf"][0][:, k0_ * nn_:k1_ * nn_].rearrange("p (k n) -> p k n", n=nn_)), B_Wdf,
              reads=[WB["wdf_bf"][1]], writes=[B_Wdf])
    for kh_ in range(2):
        k0_, k1_ = kh_ * (8 // 2), (kh_ + 1) * (8 // 2)
        nn_ = Wg.shape[2]
        T.dma("sp", M("dma_start", out=Wg[:, k0_:k1_, :],
                      in_=WB["wg_bf"][0][:, k0_ * nn_:k1_ * nn_].rearrange("p (k n) -> p k n", n=nn_)), B_Wg,
              reads=[WB["wg_bf"][1]], writes=[B_Wg])
    for kh_ in range(2):
        k0_, k1_ = kh_ * (8 // 2), (kh_ + 1) * (8 // 2)
        nn_ = Wo.shape[2]
        T.dma("sp", M("dma_start", out=Wo[:, k0_:k1_, :],
                      in_=WB["wo_bf"][0][:, k0_ * nn_:k1_ * nn_].rearrange("p (k n) -> p k n", n=nn_)), B_Wo,
              reads=[WB["wo_bf"][1]], writes=[B_Wo])

    for u in range(4):
        for j in range(4):
            xkt, B_xk = xk[j]
            r0 = 512 * u + 128 * j

            def src(xtile, B_x, r0=r0, xkt=xkt, B_xk=B_xk):
                T.dma("sp", M("dma_start", out=xtile[:, :], in_=x2_d[r0:r0 + 128, :]), B_x, writes=[B_x])
                T.dma("sp", M("dma_start", out=xkt[:, :], in_=x2_d[r0:r0 + 128, :]), B_xk, writes=[B_xk])
            norm_tile(src, 128, xnT2, B_xnT2, 128 * j)
        col = (u % 2) * 512
        cc2 = u // 2
        T.dma("pool", lambda e, cc2=cc2, col=col: e.dma_start(
            out=oT[:, :, :],
            in_=arg_d[bass.ds(((PID(e) % 4) * 2 + cc2) * 4096 + (PID(e) // 4) * 2048, 2048), col:col + 512].rearrange(
                "(k p) n -> p k n", p=128)),
            B_oT, reads=[B_arg[2 * i_ + cc2] for i_ in range(4)], writes=[B_oT])
        for blk in range(8):
            ysb, B_ysb = PB[0]
            ydf, B_ydf = PB[1]
            ga, B_ga = PB[2]
            gb, B_gb = PB[3]
            idx = 0
            for r in range(4):
                for jj in range(2):
                    hd = 2 * r + jj
                    T.op("pe", M("matmul",
                        out=ysb[:, :], lhsT=Wsb[:, hd, blk * 128:(blk + 1) * 128], rhs=oT[:, 4 * r + jj, :],
                        start=(idx == 0), stop=(idx == 7)), reads=[B_Wsb, B_oT], writes=[B_ysb])
                    idx += 1
            idx = 0
            for r in range(4):
                for jj in range(2):
                    hd = 2 * r + jj
                    T.op("pe", M("matmul",
                        out=ydf[:, :], lhsT=Wdf[:, hd, blk * 128:(blk + 1) * 128], rhs=oT[:, 4 * r + 2 + jj, :],
                        start=(idx == 0), stop=(idx == 7)), reads=[B_Wdf, B_oT], writes=[B_ydf])
                    idx += 1
            for dc in range(8):
                T.op("pe", M("matmul", out=ga[:, :], lhsT=Wg[:, dc, blk * 128:(blk + 1) * 128], rhs=xnT2[:, dc, :],
                                                     start=(dc == 0), stop=(dc == 7)), reads=[B_Wg, B_xnT2], writes=[B_ga])
            for dc in range(8):
                T.op("pe", M("matmul", out=gb[:, :], lhsT=Wg[:, dc, 1024 + blk * 128:1024 + (blk + 1) * 128],
                                                     rhs=xnT2[:, dc, :], start=(dc == 0), stop=(dc == 7)),
                     reads=[B_Wg, B_xnT2], writes=[B_gb])
            T.op("act", M("activation", out=g0[:, :], in_=ga[:, :], func=AF.Sigmoid, bias=bgate[:, blk:blk + 1]),
                 reads=[B_ga, B_bgate], writes=[B_g0])
            T.op("act", M("activation", out=g1[:, :], in_=gb[:, :], func=AF.Sigmoid, bias=bgate[:, 8 + blk:9 + blk]),
                 reads=[B_gb, B_bgate], writes=[B_g1])
            T.op("dve", M("tensor_tensor", out=g0[:, :], in0=g0[:, :], in1=ysb[:, :], op=ALU.mult), reads=[B_g0, B_ysb], writes=[B_g0])
            T.op("dve", M("tensor_tensor", out=g1[:, :], in0=g1[:, :], in1=ydf[:, :], op=ALU.mult), reads=[B_g1, B_ydf], writes=[B_g1])
            T.op("dve", M("tensor_tensor", out=mgT[:, blk, :], in0=g0[:, :], in1=g1[:, :], op=ALU.add),
                 reads=[B_g0, B_g1], writes=[B_mgT])
        for j in range(4):
            h_t, B_h = ht[j % 2]
            xkt, B_xk = xk[j]
            for half in range(2):
                mo, B_mo = PB[4 + half]
                for blk in range(8):
                    T.op("pe", M("matmul", out=mo[:, :], lhsT=mgT[:, blk, 128 * j:128 * j + 128],
                                                           rhs=Wo[:, blk, half * 512:(half + 1) * 512], start=(blk == 0), stop=(blk == 7)),
                         reads=[B_mgT, B_Wo], writes=[B_mo])
                T.op("dve", M("tensor_tensor", out=h_t[:, half * 512:(half + 1) * 512], in0=mo[:, :],
                                                             in1=xkt[:, half * 512:(half + 1) * 512], op=ALU.add),
                     reads=[B_mo, B_xk], writes=[B_h])
            r0 = 512 * u + 128 * j
            T.dma("sp", M("dma_start", out=hbuf_d[r0:r0 + 128, :], in_=h_t[:, :]), B_h,
                  reads=[B_h], writes=[B_hbuf[4 * u + j]])
    T.barrier()
    p2.close()

    p3 = contextlib.ExitStack()
    Wfg, B_Wfg = sb("Wfg", [128, 8, DFF], BF, p3)
    Wfu, B_Wfu = sb("Wfu", [128, 8, DFF], BF, p3)
    Wfd, B_Wfd = sb("Wfd", [128, NFB, 1024], BF, p3)
    gfin, B_gfin = sb("gfin", [128, D], F32, p3)
    hnT, B_hnT = sb("hnT", [128, 8, 256], BF, p3)
    hidT, B_hidT = sb("hidT", [128, NFB, 256], BF, p3)
    hk = [sb("hk%d" % j, [128, D], F32, p3) for j in range(2)]
    sg, B_sg = sb("sg", [128, 256], F32, p3)
    yo = [sb("yo%d" % j, [128, D], F32, p3) for j in range(2)]
    T.dma("sp", M("dma_start", out=gfin[:, :], in_=gfin_d[:, :]), B_gfin, writes=[B_gfin])
    for kh_ in range(2):
        k0_, k1_ = kh_ * (8 // 2), (kh_ + 1) * (8 // 2)
        nn_ = Wfg.shape[2]
        T.dma("sp", M("dma_start", out=Wfg[:, k0_:k1_, :],
                      in_=WB["wfg_bf"][0][:, k0_ * nn_:k1_ * nn_].rearrange("p (k n) -> p k n", n=nn_)), B_Wfg,
              reads=[WB["wfg_bf"][1]], writes=[B_Wfg])
    for kh_ in range(2):
        k0_, k1_ = kh_ * (8 // 2), (kh_ + 1) * (8 // 2)
        nn_ = Wfu.shape[2]
        T.dma("sp", M("dma_start", out=Wfu[:, k0_:k1_, :],
                      in_=WB["wfu_bf"][0][:, k0_ * nn_:k1_ * nn_].rearrange("p (k n) -> p k n", n=nn_)), B_Wfu,
              reads=[WB["wfu_bf"][1]], writes=[B_Wfu])
    for kh_ in range(2):
        k0_, k1_ = kh_ * (22 // 2), (kh_ + 1) * (22 // 2)
        nn_ = Wfd.shape[2]
        T.dma("sp", M("dma_start", out=Wfd[:, k0_:k1_, :],
                      in_=WB["wfd_bf"][0][:, k0_ * nn_:k1_ * nn_].rearrange("p (k n) -> p k n", n=nn_)), B_Wfd,
              reads=[WB["wfd_bf"][1]], writes=[B_Wfd])

    for v in range(8):
        for j in range(2):
            hkt, B_hk = hk[j]
            r0 = 256 * v + 128 * j

            def src(xtile, B_x, r0=r0, hkt=hkt, B_hk=B_hk, v=v, j=j):
                T.dma("sp", M("dma_start", out=xtile[:, :], in_=hbuf_d[r0:r0 + 128, :]), B_x,
                      reads=[B_hbuf[2 * v + j]], writes=[B_x])
                T.dma("sp", M("dma_start", out=hkt[:, :], in_=hbuf_d[r0:r0 + 128, :]), B_hk,
                      reads=[B_hbuf[2 * v + j]], writes=[B_hk])
            norm_tile(src, 128, hnT, B_hnT, 128 * j)
        for fb in range(NFB):
            gp, B_gp = PB[fb % 2]
            up, B_up = PB[2 + fb % 2]
            for dc in range(8):
                T.op("pe", M("matmul", out=gp[:, 0:256], lhsT=Wfg[:, dc, fb * 128:(fb + 1) * 128], rhs=hnT[:, dc, :],
                                                            start=(dc == 0), stop=(dc == 7)), reads=[B_Wfg, B_hnT], writes=[B_gp])
            for dc in range(8):
                T.op("pe", M("matmul", out=up[:, 0:256], lhsT=Wfu[:, dc, fb * 128:(fb + 1) * 128], rhs=hnT[:, dc, :],
                                                            start=(dc == 0), stop=(dc == 7)), reads=[B_Wfu, B_hnT], writes=[B_up])
            T.op("act", M("activation", out=sg[:, :], in_=gp[:, 0:256], func=AF.Silu), reads=[B_gp], writes=[B_sg])
            T.op("dve", M("tensor_tensor", out=hidT[:, fb, :], in0=sg[:, :], in1=up[:, 0:256], op=ALU.mult),
                 reads=[B_sg, B_up], writes=[B_hidT])
        for j in range(2):
            hkt, B_hk = hk[j]
            yt, B_yt = yo[j]
            for half in range(2):
                dn, B_dn = PB[4 + half]
                for fb in range(NFB):
                    T.op("pe", M("matmul", out=dn[:, :], lhsT=hidT[:, fb, 128 * j:128 * j + 128],
                                                                rhs=Wfd[:, fb, half * 512:(half + 1) * 512],
                                                                start=(fb == 0), stop=(fb == NFB - 1)),
                         reads=[B_hidT, B_Wfd], writes=[B_dn])
                T.op("dve", M("tensor_tensor", out=hkt[:, half * 512:(half + 1) * 512], in0=dn[:, :],
                                                             in1=hkt[:, half * 512:(half + 1) * 512], op=ALU.add),
                     reads=[B_dn, B_hk], writes=[B_hk])
            T.op("act", M("activation", out=sqj[:, :], in_=hkt[:, :], func=AF.Square, accum_out=stat[:, 4:5]),
                 reads=[B_hk], writes=[B_sqj, B_stat])
            T.op("act", M("activation", out=stat[:, 5:6], in_=stat[:, 4:5], func=AF.Ln, scale=1.0 / D, bias=cf[:, 280:281]),
                 reads=[B_stat, B_cf], writes=[B_stat])
            T.op("act", M("activation", out=stat[:, 6:7], in_=stat[:, 5:6], func=AF.Exp, scale=-0.5), reads=[B_stat], writes=[B_stat])
            T.op("dve", M("scalar_tensor_tensor", out=yt[:, :], in0=hkt[:, :], scalar=stat[:, 6:7], in1=gfin[:, :],
                                                         op0=ALU.mult, op1=ALU.mult), reads=[B_hk, B_stat, B_gfin], writes=[B_yt])
            r0 = 256 * v + 128 * j
            T.dma("sp", M("dma_start", out=y_d[r0:r0 + 128, :], in_=yt[:, :]), B_yt, reads=[B_yt], writes=[B_y])
    T.barrier()

    finish()
    p3.close()
    es.close()
    return nc


_NC = {}


def kernel(x, meta, norm_mix_g, w_in, w_gate, b_gate, lam_q1, lam_k1, lam_q2, lam_k2, subln_g,
           w_br_sb, w_br_df, w_out, norm_ffn_g, w_ffn_gate, w_ffn_up, w_ffn_down, norm_final_g):
    in_maps = prep(x, meta, norm_mix_g, w_in, w_gate, b_gate, lam_q1, lam_k1, lam_q2, lam_k2, subln_g,
                   w_br_sb, w_br_df, w_out, norm_ffn_g, w_ffn_gate, w_ffn_up, w_ffn_down, norm_final_g)
    if "nc" not in _NC:
        _NC["nc"] = build()
    nc = _NC["nc"]
    res = run_bass_kernel_spmd(nc, in_maps, core_ids=list(range(8)))
    out = np.empty((2, S, D), np.float32)
    for c in range(8):
        b, g = c // 4, c % 4
        out[b, 2048 * g:2048 * (g + 1)] = res.results[c]["y"]
    return out


def prep(x, meta, norm_mix_g, w_in, w_gate, b_gate, lam_q1, lam_k1, lam_q2, lam_k2, subln_g,
         w_br_sb, w_br_df, w_out, norm_ffn_g, w_ffn_gate, w_ffn_up, w_ffn_down, norm_final_g):
    f = lambda a: np.ascontiguousarray(np.asarray(a, dtype=np.float32))
    x = f(x)
    w_in = f(w_in)[0]
    lam = np.concatenate([f(lam_q1)[0], f(lam_k1)[0], f(lam_q2)[0], f(lam_k2)[0]])
    lam_b = np.ascontiguousarray(np.broadcast_to(lam[None, :], (128, 256)))
    gfin_b = np.ascontiguousarray(np.broadcast_to(f(norm_final_g)[None, :], (128, D)))
    common = {
        "meta": f(meta), "gmix": _pk(f(norm_mix_g)[0], 8), "wgate": f(w_gate)[0], "bgate": _pk(f(b_gate)[0], 16),
        "wsb": f(w_br_sb)[0], "wdf": f(w_br_df)[0], "wout": f(w_out)[0], "gffn": _pk(f(norm_ffn_g)[0], 8),
        "wfg": f(w_ffn_gate)[0], "wfu": f(w_ffn_up)[0], "wfd": f(w_ffn_down)[0], "gfin": gfin_b, "lam": lam_b,
        "subg": f(subln_g)[0].reshape(128, 1).copy(),
    }
    in_maps = []
    for c in range(8):
        b, g = c // 4, c % 4
        cols = []
        for h in (2 * g, 2 * g + 1):
            cols += [np.arange(h * 128, (h + 1) * 128), np.arange(1024 + h * 128, 1024 + (h + 1) * 128)]
        for h in (2 * g, 2 * g + 1):
            cols += [np.arange(3072 + h * 128, 3072 + (h + 1) * 128), np.arange(4096 + h * 128, 4096 + (h + 1) * 128)]
        for h in (2 * g, 2 * g + 1):
            cols += [np.arange(2048 + h * 128, 2048 + (h + 1) * 128)]
        for h in (2 * g, 2 * g + 1):
            cols += [np.arange(5120 + h * 128, 5120 + (h + 1) * 128)]
        w1 = np.ascontiguousarray(w_in[:, np.concatenate(cols)])
        cbv, c3v, cfv, kbv = _consts(g)
        m = dict(common)
        m.update({"xb": x[b], "x2": np.ascontiguousarray(x[b, 2048 * g:2048 * (g + 1)]), "w1": w1,
                  "cb": cbv, "c3": c3v, "cf": cfv, "kb": kbv})
        in_maps.append(m)
    return in_maps
```
